# Optimizing a Trainium2 kernel written in Bass

```python
import jax, jax.numpy as jnp
from jax import lax
import numpy as np

D_MODEL = 1024
BATCH = 16
SEQ = 4096
DEPTH = 4
DEC_BATCH = 8
DEC_SEQ = 16
PAST_LEN = 1024

CHUNK = 64
D_MIX = D_MODEL
SSD_WIDTH = D_MIX // 2
SSD_HEAD_DIM = 64
SSD_HEADS = SSD_WIDTH // SSD_HEAD_DIM
SSD_GROUPS = 2
SSD_HPG = SSD_HEADS // SSD_GROUPS
D_STATE = 128
CONV_W = 4
CONV_DIM = SSD_WIDTH + 2 * SSD_GROUPS * D_STATE
SSD_CHUNK = CHUNK
CMLP_WIDTH = D_MIX - SSD_WIDTH
CMLP_GROUPS = 4
CMLP_GDIM = CMLP_WIDTH // CMLP_GROUPS
CMLP_CHUNK = 128
D_FF = 4 * D_MODEL
N_MOD = 6
IN_DIM = SSD_WIDTH + CONV_DIM + SSD_HEADS + 2 * CMLP_WIDTH
RMS_EPS = 1e-6
LN_EPS = 1e-5

kernel_name = 'hybrid_ssd_chunkmlp_stream_step'


def rms_norm(x, g):
    xf = x.astype(jnp.float32)
    y = xf * lax.rsqrt(jnp.mean(xf * xf, axis=-1, keepdims=True) + RMS_EPS)
    return (y * g.astype(jnp.float32)).astype(x.dtype)


def layer_norm(x, g, b):
    xf = x.astype(jnp.float32)
    mu = jnp.mean(xf, axis=-1, keepdims=True)
    xc = xf - mu
    y = xc * lax.rsqrt(jnp.mean(xc * xc, axis=-1, keepdims=True) + LN_EPS)
    return (y * g.astype(jnp.float32) + b.astype(jnp.float32)).astype(x.dtype)


def causal_conv(xbc, cache, w, b):
    L = xbc.shape[1]
    xp = jnp.concatenate([cache.astype(xbc.dtype), xbc], axis=1)
    out = b
    for k in range(CONV_W):
        out = out + w[k] * xp[:, k:k + L]
    return jax.nn.silu(out), xp[:, -(CONV_W - 1):]


def ssd_scan(x, dt, A, Bm, Cm, h0):
    b, L = x.shape[:2]
    Q = min(SSD_CHUNK, L)
    nc = L // Q
    x = x.reshape(b, nc, Q, SSD_GROUPS, SSD_HPG, SSD_HEAD_DIM)
    dt = dt.reshape(b, nc, Q, SSD_GROUPS, SSD_HPG)
    Bm = Bm.reshape(b, nc, Q, SSD_GROUPS, D_STATE)
    Cm = Cm.reshape(b, nc, Q, SSD_GROUPS, D_STATE)
    a_cum = jnp.cumsum(dt * A.reshape(SSD_GROUPS, SSD_HPG), axis=2)
    seg = a_cum[:, :, :, None] - a_cum[:, :, None, :]
    mask = jnp.tril(jnp.ones((Q, Q), dtype=bool))[:, :, None, None]
    decay = jnp.exp(jnp.where(mask, seg, -jnp.inf))
    cb = jnp.einsum('bcign,bcjgn->bcijg', Cm, Bm)
    xdt = x * dt[..., None]
    y_diag = jnp.einsum('bcijg,bcijgr,bcjgrp->bcigrp', cb, decay, xdt)
    a_last = a_cum[:, :, -1]
    w_end = jnp.exp(a_last[:, :, None] - a_cum) * dt
    states = jnp.einsum('bcjgn,bcjgr,bcjgrp->bcgrpn', Bm, w_end, x)

    def step(h, inp):
        s, al = inp
        return jnp.exp(al)[..., None, None] * h + s, h

    h0 = h0.reshape(b, SSD_GROUPS, SSD_HPG, SSD_HEAD_DIM, D_STATE)
    h_fin, h_in = lax.scan(step, h0, (jnp.moveaxis(states, 1, 0), jnp.moveaxis(a_last, 1, 0)))
    h_in = jnp.moveaxis(h_in, 0, 1)
    y_off = jnp.einsum('bcign,bcgrpn,bcigr->bcigrp', Cm, h_in, jnp.exp(a_cum))
    y = (y_diag + y_off).reshape(b, L, SSD_HEADS, SSD_HEAD_DIM)
    return y, h_fin.reshape(b, SSD_HEADS, SSD_HEAD_DIM, D_STATE)


def mixer(h, conv_cache, ssd_h0, w_in, conv_w, conv_b, dt_bias, a_log, d_skip,
          ssd_norm_g, v_ln_g, v_ln_b, w_s, b_s, w_out):
    b, L, _ = h.shape
    f32 = jnp.float32
    proj = h @ w_in
    z, xbc, dt_raw, uv = jnp.split(
        proj, [SSD_WIDTH, SSD_WIDTH + CONV_DIM, SSD_WIDTH + CONV_DIM + SSD_HEADS], axis=-1)
    xbc_act, conv_state = causal_conv(xbc, conv_cache, conv_w, conv_b)
    xs, Bm, Cm = jnp.split(xbc_act, [SSD_WIDTH, SSD_WIDTH + SSD_GROUPS * D_STATE], axis=-1)
    dt = jax.nn.softplus(dt_raw.astype(f32) + dt_bias.astype(f32))
    A = -jnp.exp(a_log.astype(f32))
    xh = xs.astype(f32).reshape(b, L, SSD_HEADS, SSD_HEAD_DIM)
    y, h_fin = ssd_scan(xh, dt, A,
                        Bm.astype(f32).reshape(b, L, SSD_GROUPS, D_STATE),
                        Cm.astype(f32).reshape(b, L, SSD_GROUPS, D_STATE),
                        ssd_h0.astype(f32))
    y = y + d_skip.astype(f32)[:, None] * xh
    y = y.reshape(b, L, SSD_WIDTH) * jax.nn.silu(z.astype(f32))
    y_ssd = rms_norm(y, ssd_norm_g).astype(h.dtype)
    u, v = jnp.split(jax.nn.gelu(uv), 2, axis=-1)
    v = layer_norm(v, v_ln_g, v_ln_b)
    Q = min(CMLP_CHUNK, L)
    nk = L // Q
    vq = v.reshape(b, nk, Q, CMLP_GROUPS, CMLP_GDIM)
    ws = jnp.tril(w_s[:, :Q, :Q])
    mixed = jnp.einsum('gij,bkjgd->bkigd', ws, vq) + b_s[:, :Q].T[None, None, :, :, None]
    y_cmlp = (u.reshape(b, nk, Q, CMLP_GROUPS, CMLP_GDIM) * mixed).reshape(b, L, CMLP_WIDTH)
    out = jnp.concatenate([y_ssd, y_cmlp.astype(h.dtype)], axis=-1) @ w_out
    return out, conv_state, h_fin, v


def block(x, c, conv_cache, ssd_h0, w_mod, b_mod, norm_g, w_in, conv_w, conv_b, dt_bias,
          a_log, d_skip, ssd_norm_g, v_ln_g, v_ln_b, w_s, b_s, w_out, w_ff1, w_ff2):
    mod = (jax.nn.silu(c) @ w_mod + b_mod)[:, None, :]
    sh1, sc1, g1, sh2, sc2, g2 = jnp.split(mod, N_MOD, axis=-1)
    h = rms_norm(x, norm_g[0]) * (1 + sc1) + sh1
    m, conv_state, h_fin, v = mixer(h, conv_cache, ssd_h0, w_in, conv_w, conv_b, dt_bias,
                                    a_log, d_skip, ssd_norm_g, v_ln_g, v_ln_b, w_s, b_s, w_out)
    x = x + g1 * rms_norm(m, norm_g[1])
    h = rms_norm(x, norm_g[2]) * (1 + sc2) + sh2
    f = jnp.square(jax.nn.relu(h @ w_ff1)) @ w_ff2
    x = x + g2 * rms_norm(f, norm_g[3])
    return x, conv_state, h_fin.astype(x.dtype), v


def setup_inputs(seed: int = 0) -> dict:
    key = jax.random.key(seed)
    ks = jax.random.split(key, 24)
    nrm = jax.random.normal
    dt0 = jnp.exp(jax.random.uniform(ks[10], (DEPTH, SSD_HEADS), minval=np.log(1e-3), maxval=np.log(1e-1)))
    return {
        'x_prompt': nrm(ks[0], (BATCH, SEQ, D_MODEL), jnp.float32),
        'x_sample': nrm(ks[1], (DEC_BATCH, DEC_SEQ, D_MODEL), jnp.float32),
        'state_conv': nrm(ks[2], (DEPTH, DEC_BATCH, CONV_W - 1, CONV_DIM), jnp.float32),
        'state_ssd': 0.1 * nrm(ks[3], (DEPTH, DEC_BATCH, SSD_HEADS, SSD_HEAD_DIM, D_STATE), jnp.float32),
        'c_prompt': nrm(ks[4], (BATCH, D_MODEL), jnp.float32),
        'c_sample': nrm(ks[5], (DEC_BATCH, D_MODEL), jnp.float32),
        'w_mod': 0.5 * D_MODEL ** -0.5 * nrm(ks[6], (DEPTH, D_MODEL, N_MOD * D_MODEL), jnp.float32),
        'b_mod': 0.01 * nrm(ks[7], (DEPTH, N_MOD * D_MODEL), jnp.float32),
        'norm_g': 1.0 + 0.02 * nrm(ks[8], (DEPTH, 4, D_MODEL), jnp.float32),
        'w_in': D_MODEL ** -0.5 * nrm(ks[9], (DEPTH, D_MODEL, IN_DIM), jnp.float32),
        'conv_w': CONV_W ** -0.5 * nrm(ks[11], (DEPTH, CONV_W, CONV_DIM), jnp.float32),
        'conv_b': 0.02 * nrm(ks[12], (DEPTH, CONV_DIM), jnp.float32),
        'dt_bias': dt0 + jnp.log(-jnp.expm1(-dt0)),
        'a_log': jnp.log(jax.random.uniform(ks[13], (DEPTH, SSD_HEADS), minval=1.0, maxval=16.0)),
        'd_skip': 1.0 + 0.1 * nrm(ks[14], (DEPTH, SSD_HEADS), jnp.float32),
        'ssd_norm_g': 1.0 + 0.02 * nrm(ks[15], (DEPTH, SSD_WIDTH), jnp.float32),
        'v_ln_g': 1.0 + 0.02 * nrm(ks[16], (DEPTH, CMLP_WIDTH), jnp.float32),
        'v_ln_b': 0.02 * nrm(ks[17], (DEPTH, CMLP_WIDTH), jnp.float32),
        'w_s': CMLP_CHUNK ** -0.5 * nrm(ks[18], (DEPTH, CMLP_GROUPS, CMLP_CHUNK, CMLP_CHUNK), jnp.float32),
        'b_s': 1.0 + 0.02 * nrm(ks[19], (DEPTH, CMLP_GROUPS, CMLP_CHUNK), jnp.float32),
        'w_out': D_MIX ** -0.5 * nrm(ks[20], (DEPTH, D_MIX, D_MODEL), jnp.float32),
        'w_ff1': D_MODEL ** -0.5 * nrm(ks[21], (DEPTH, D_MODEL, D_FF), jnp.float32),
        'w_ff2': D_FF ** -0.5 * nrm(ks[22], (DEPTH, D_FF, D_MODEL), jnp.float32),
    }


def reference(x_prompt, x_sample, state_conv, state_ssd, c_prompt, c_sample, w_mod, b_mod,
              norm_g, w_in, conv_w, conv_b, dt_bias, a_log, d_skip, ssd_norm_g, v_ln_g,
              v_ln_b, w_s, b_s, w_out, w_ff1, w_ff2):
    bp = x_prompt.shape[0]
    conv_p0 = jnp.zeros((bp, CONV_W - 1, CONV_DIM), x_prompt.dtype)
    ssd_p0 = jnp.zeros((bp, SSD_HEADS, SSD_HEAD_DIM, D_STATE), jnp.float32)
    xp, xs = x_prompt, x_sample
    conv_p, ssd_p, conv_s, ssd_s, v_s = [], [], [], [], []
    for l in range(DEPTH):
        lp = [w[l] for w in (w_mod, b_mod, norm_g, w_in, conv_w, conv_b, dt_bias, a_log, d_skip,
                             ssd_norm_g, v_ln_g, v_ln_b, w_s, b_s, w_out, w_ff1, w_ff2)]
        xp, cp, hp, _ = block(xp, c_prompt, conv_p0, ssd_p0, *lp)
        xs, cs, hs, vs = block(xs, c_sample, state_conv[l], state_ssd[l], *lp)
        conv_p.append(cp)
        ssd_p.append(hp)
        conv_s.append(cs)
        ssd_s.append(hs)
        v_s.append(vs)
    return (xp, xs, jnp.stack(conv_p), jnp.stack(ssd_p), jnp.stack(conv_s), jnp.stack(ssd_s), jnp.stack(v_s))
```

```python
import numpy as np
import concourse.bass as bass
import concourse.mybir as mybir
from concourse.bass_utils import run_bass_kernel_spmd

F32 = mybir.dt.float32
BF16 = mybir.dt.bfloat16
U8 = mybir.dt.uint8
AF = mybir.ActivationFunctionType
ALU = mybir.AluOpType

D = 1024
KC = 8
SSDW = 512
CONVD = 1024
NH = 8
HD = 64
NST = 128
CW = 512
DFF = 4096
IN_DIM = 2568
NMOD = 6
RMS_EPS = 1e-6
LN_EPS = 1e-5
NEG = -30000.0


def dsize(dt):
    return mybir.dt.size(dt)


class Cfg:
    def __init__(self, depth=4, nps=2, seq=4096, dseq=16, ncores=8, T=256):
        self.depth, self.nps, self.seq, self.dseq, self.ncores, self.T = depth, nps, seq, dseq, ncores, T


class Op:
    __slots__ = ("eng", "fn", "deps", "is_dma", "sig", "sem", "val", "idx")


class Sched:
    ENG = ("pe", "act", "dve", "pool", "sp")

    def __init__(self, nc, tracked_dram=()):
        self.nc = nc
        self.ops = []
        self.regions = {}
        self.psum_last = {}
        self.tracked_dram = set(tracked_dram)
        self.dma_slots = {"sp": 16, "pool": 8, "act": 4}
        self.dma_count = {"sp": 0, "pool": 0, "act": 0}
        self.dma_slot_last = {}
        self.out_dmas = []

    def _rng(self, ap):
        sp = str(ap.space)
        name = ap.tensor.name
        if "DRAM" in sp:
            if name in self.tracked_dram:
                ext = 1
                for st_, c_ in ap.ap:
                    ext += (c_ - 1) * abs(st_)
                return ("d", name, ap.offset, ap.offset + ext, 0, 128)
            return None
        esz = dsize(ap.dtype)
        pairs = ap.ap
        pstep, pcnt = pairs[0]
        off = ap.offset
        if pstep > 0:
            p0 = off // pstep
            foff = off % pstep
        else:
            p0, foff = 0, off
        if "PSUM" in sp:
            return ("p", name, 0, 0, 0, 0)
        ext = 1
        for s, c in pairs[1:]:
            ext += (c - 1) * abs(s)
        lo = foff * esz
        return ("s", name, lo, lo + ext * esz, p0, p0 + pcnt)

    def add(self, eng, fn, reads=(), writes=(), is_dma=False, is_out=False):
        op = Op()
        op.eng, op.fn, op.is_dma, op.sig, op.sem, op.val = eng, fn, is_dma, False, None, 0
        op.idx = len(self.ops)
        deps = {}

        def dep(i, kind):
            if i is None or i == op.idx:
                return
            k = deps.get(i)
            if k is None or kind == "raw":
                deps[i] = kind

        for ap, is_w in [(a, False) for a in reads] + [(a, True) for a in writes]:
            r = self._rng(ap)
            if r is None:
                continue
            kind, name, lo, hi, plo, phi = r
            if kind == "p":
                last = self.psum_last.setdefault(name, {})
                for e, i in last.items():
                    if e == eng and eng == "pe":
                        continue
                    dep(i, "raw")
                continue
            lst = self.regions.setdefault(name, [])
            keep = []
            for ent in lst:
                elo, ehi, eplo, ephi, ei, ew, eeng, edma = ent
                ov = not (ehi <= lo or hi <= elo or ephi <= plo or phi <= eplo)
                if ov:
                    if is_w:
                        if not (kind == "d" and ew):
                            dep(ei, "waw" if ew else "war")
                    elif ew:
                        dep(ei, "raw")
                if kind != "d" and is_w and ov and elo >= lo and ehi <= hi and eplo >= plo and ephi <= phi:
                    continue
                if (not is_w) and (not ew) and eeng == eng and (not edma) and (not is_dma) \
                        and elo == lo and ehi == hi and eplo == plo and ephi == phi:
                    continue
                keep.append(ent)
            keep.append((lo, hi, plo, phi, op.idx, is_w, eng, is_dma))
            self.regions[name] = keep
        for ap in list(reads) + list(writes):
            r = self._rng(ap)
            if r is not None and r[0] == "p":
                self.psum_last[r[1]][eng] = op.idx
        if is_dma:
            n = self.dma_count[eng]
            self.dma_count[eng] = n + 1
            slot = (eng, n % self.dma_slots[eng])
            prev = self.dma_slot_last.get(slot)
            if prev is not None:
                deps[prev] = "slot"
            self.dma_slot_last[slot] = op.idx
            op.sem = slot
            op.sig = True
            if is_out:
                self.out_dmas.append(op.idx)
        best = {}
        red = {}
        for i, kind in deps.items():
            d = self.ops[i]
            if d.is_dma:
                red[i] = kind
                continue
            cur = best.get(d.eng)
            if cur is None:
                best[d.eng] = [i, kind]
            else:
                if i > cur[0]:
                    cur[0] = i
                if kind == "raw":
                    cur[1] = "raw"
        for e, (i, kind) in best.items():
            red[i] = kind
        op.deps = red
        self.ops.append(op)
        return op

    def emit(self, sems):
        nc = self.nc
        H = {"pe": nc.tensor, "act": nc.scalar, "dve": nc.vector, "pool": nc.gpsimd, "sp": nc.sync}
        ops = self.ops
        need = []
        for op in ops:
            w = []
            for i, kind in op.deps.items():
                d = ops[i]
                if d.is_dma:
                    w.append(i)
                elif op.is_dma:
                    w.append(i)
                elif d.eng != op.eng:
                    w.append(i)
                else:
                    if op.eng != "pe" and kind == "raw":
                        w.append(i)
            for i in w:
                ops[i].sig = True
            need.append(w)
        cnt = {e: 0 for e in self.ENG}
        slot_cnt = {}
        for op in ops:
            if not op.sig:
                continue
            if op.is_dma:
                c = slot_cnt.get(op.sem, 0) + 1
                slot_cnt[op.sem] = c
                op.val = 16 * c
            else:
                cnt[op.eng] += 1
                op.sem = ("c", op.eng)
                op.val = cnt[op.eng]
        waited = {e: {} for e in self.ENG}
        nwait = 0
        for op, w in zip(ops, need):
            h = H[op.eng]
            wd = waited[op.eng]
            tgt = {}
            for i in w:
                d = ops[i]
                if wd.get(d.sem, 0) < d.val and tgt.get(d.sem, 0) < d.val:
                    tgt[d.sem] = d.val
            for s, v in tgt.items():
                h.wait_ge(sems[s], v)
                wd[s] = v
                nwait += 1
            inst = op.fn()
            if op.sig:
                inst.then_inc(sems[op.sem], 16 if op.is_dma else 1)
        for s, c in slot_cnt.items():
            nc.sync.wait_ge(sems[s], 16 * c)
        for e in ("pe", "act", "dve", "pool"):
            if cnt[e] > 0:
                nc.sync.wait_ge(sems[("c", e)], cnt[e])
        self.stats = dict(nops=len(ops), nwait=nwait, cnt=cnt)


class Arena:
    def __init__(self, nc, nbytes):
        self.nb = nbytes
        self.ap = nc.alloc_sbuf_tensor("arena", [128, nbytes], U8).ap()
        self.off = 0
        self.marks = {}

    def alloc(self, shape, dtype, at=None, name=None):
        n = 1
        for s in shape:
            n *= s
        nb = n * dsize(dtype)
        if at is None:
            off = (self.off + 31) // 32 * 32
            self.off = off + nb
            assert self.off <= self.nb, ("SBUF arena overflow", name, self.off, self.nb)
        else:
            off = at
            assert off + nb <= self.nb
        if name:
            self.marks[name] = (off, nb)
        v = self.ap[:, off:off + nb].bitcast(dtype)
        if len(shape) == 2:
            v = v.rearrange("p (a b) -> p a b", a=shape[0])
        elif len(shape) == 3:
            v = v.rearrange("p (a b c) -> p a b c", a=shape[0], b=shape[1])
        elif len(shape) == 4:
            v = v.rearrange("p (a b c d) -> p a b c d", a=shape[0], b=shape[1], c=shape[2])
        return v


def consts_host():
    c = np.zeros((128, 512), np.float32)
    j = np.arange(128)[:, None]
    i = np.arange(128)[None, :]
    c[:, 0:128] = np.eye(128, dtype=np.float32)
    c[:, 128:256] = (j <= i).astype(np.float32)
    c[:, 256:384] = np.where(i >= j, 0.0, NEG)
    c[:, 384:512] = 1.0
    return c


def build(cfg):
    L, NPS, SEQ, DSQ, T = cfg.depth, cfg.nps, cfg.seq, cfg.dseq, cfg.T
    nc = bass.Bass("TRN2", target_bir_lowering=False)

    def din(name, shape, dt=F32):
        return nc.dram_tensor(name, list(shape), dt, kind="ExternalInput").ap()

    def dout(name, shape, dt=F32):
        return nc.dram_tensor(name, list(shape), dt, kind="ExternalOutput").ap()

    def dscr(name, shape, dt=BF16):
        return nc.dram_tensor(name, list(shape), dt, kind="Internal").ap()

    x_prompt = din("x_prompt", [NPS, SEQ, D])
    x_sample = din("x_sample", [1, DSQ, D])
    state_conv = din("state_conv", [L, 1, 3, CONVD])
    state_ssd = din("state_ssd", [L, 1, NH * HD, NST])
    cvec = din("cvec", [NPS + 1, D])
    consts = din("consts", [128, 512])
    w_mod = din("w_mod", [L, D, NMOD * D])
    b_mod = din("b_mod", [L, NMOD * D])
    norm_g = din("norm_g", [L, 4, D])
    w_in = din("w_in", [L, D, IN_DIM])
    conv_w = din("conv_w", [L, 4, CONVD])
    conv_b = din("conv_b", [L, CONVD])
    dt_bias = din("dt_bias", [L, NH])
    a_log = din("a_log", [L, NH])
    d_skip = din("d_skip", [L, NH])
    ssd_norm_g = din("ssd_norm_g", [L, SSDW])
    v_ln_g = din("v_ln_g", [L, CW])
    v_ln_b = din("v_ln_b", [L, CW])
    w_s = din("w_s", [L, 4, 128, 128])
    b_s = din("b_s", [L, 4 * 128])
    w_out = din("w_out", [L, D, D])
    w_ff1 = din("w_ff1", [L, D, DFF])
    w_ff2 = din("w_ff2", [L, DFF, D])

    y_prompt = dout("y_prompt", [NPS, SEQ, D])
    y_sample = dout("y_sample", [1, DSQ, D])
    conv_prompt = dout("conv_prompt", [L, NPS, 3, CONVD])
    ssd_prompt = dout("ssd_prompt", [L, NPS, NH * HD, NST])
    conv_sample = dout("conv_sample", [L, 1, 3, CONVD])
    ssd_sample = dout("ssd_sample", [L, 1, NH * HD, NST])
    v_sample = dout("v_sample", [L, 1, DSQ, CW])

    NBLK_IN, NBLK_OUT, NBLK_F1, NBLK_F2 = 5, 2, 8, 8
    BLK = 128 * 4096
    s_in = [dscr("s_in%d" % l, [NBLK_IN, 128, 4096]) for l in range(L)]
    s_out = [dscr("s_out%d" % l, [NBLK_OUT, 128, 4096]) for l in range(L)]
    s_f1 = [dscr("s_f1%d" % l, [NBLK_F1, 128, 4096]) for l in range(L)]
    s_f2 = [dscr("s_f2%d" % l, [NBLK_F2, 128, 4096]) for l in range(L)]
    tracked = [t.tensor.name for t in s_in + s_out + s_f1 + s_f2]

    S = Sched(nc, tracked_dram=tracked)
    A = Arena(nc, 206 * 1024)
    banks = [nc.alloc_psum_tensor("pb%d" % i, [128, 2048], U8).ap() for i in range(8)]
    bank_i = [0]

    def psum(dt=F32):
        b = banks[bank_i[0] % 8]
        bank_i[0] += 1
        return b.bitcast(dt)

    def mm(out, lhsT, rhs, start=True, stop=True):
        return S.add("pe", lambda: nc.tensor.matmul(out, lhsT=lhsT, rhs=rhs, start=start, stop=stop),
                     reads=[lhsT, rhs], writes=[out])

    def tr(out, in_, ident):
        return S.add("pe", lambda: nc.tensor.transpose(out, in_, ident), reads=[in_, ident], writes=[out])

    def act(out, in_, func, bias=None, scale=None, accum=None):
        kw = {}
        rd = [in_]
        wr = [out]
        if bias is not None:
            kw["bias"] = bias
            if not isinstance(bias, (int, float)):
                rd.append(bias)
        if scale is not None:
            kw["scale"] = scale
            if not isinstance(scale, (int, float)):
                rd.append(scale)
        if accum is not None:
            kw["accum_out"] = accum
            wr.append(accum)
        return S.add("act", lambda: nc.scalar.activation(out=out, in_=in_, func=func, **kw), reads=rd, writes=wr)

    def _eng(e):
        return nc.vector if e == "dve" else nc.gpsimd

    def tt(e, out, in0, in1, op):
        return S.add(e, lambda: _eng(e).tensor_tensor(out=out, in0=in0, in1=in1, op=op), reads=[in0, in1], writes=[out])

    def ts(e, out, in0, s1, s2, op0, op1=None):
        rd = [in0] + [s for s in (s1, s2) if s is not None and not isinstance(s, (int, float))]
        if op1 is None:
            return S.add(e, lambda: _eng(e).tensor_scalar(out=out, in0=in0, scalar1=s1, scalar2=None, op0=op0),
                         reads=rd, writes=[out])
        return S.add(e, lambda: _eng(e).tensor_scalar(out=out, in0=in0, scalar1=s1, scalar2=s2, op0=op0, op1=op1),
                     reads=rd, writes=[out])

    def stt(e, out, in0, scalar, in1, op0, op1):
        rd = [in0, in1] + ([scalar] if not isinstance(scalar, (int, float)) else [])
        return S.add(e, lambda: _eng(e).scalar_tensor_tensor(out=out, in0=in0, scalar=scalar, in1=in1, op0=op0, op1=op1),
                     reads=rd, writes=[out])

    def cp(e, out, in_):
        if e == "act":
            return act(out, in_, AF.Copy)
        return S.add(e, lambda: _eng(e).tensor_copy(out=out, in_=in_), reads=[in_], writes=[out])

    def mset(e, ap, val):
        return S.add(e, lambda: _eng(e).memset(ap, val), writes=[ap])

    def dma(q, out, in_, slow=False, is_out=False):
        h = {"sp": nc.sync, "pool": nc.gpsimd, "act": nc.scalar}[q]
        if slow:
            fn = lambda: h.dma_start(out=out, in_=in_, allow_slow_non_contiguous=True)
        else:
            fn = lambda: h.dma_start(out=out, in_=in_)
        return S.add(q, fn, reads=[in_], writes=[out], is_dma=True, is_out=is_out)

    def bc_row(dram_ap_row, n):
        return bass.AP(dram_ap_row.tensor, dram_ap_row.offset, [[0, 128], [1, n]])

    NS_MAX = max(T // 128, 1)
    cf = A.alloc([512], F32, name="cf")
    ident_f, tri_f, negm_f, ones_f = cf[:, 0:128], cf[:, 128:256], cf[:, 256:384], cf[:, 384:512]
    cb = A.alloc([5, 128], BF16, name="cb")
    ident_b, tri_b, negm_b, ones_b, ntri_b = cb[:, 0, :], cb[:, 1, :], cb[:, 2, :], cb[:, 3, :], cb[:, 4, :]
    epsb = A.alloc([4], F32)
    dtb_bc = A.alloc([L, NH], F32)
    apos_bc = A.alloc([L, NH], F32)
    dsk_bc = A.alloc([L, NH], F32)
    convw = A.alloc([L, 8, 4], F32)
    convb = A.alloc([L, 8], F32)
    ng = A.alloc([L, 4, 8], F32)
    bmod = A.alloc([L, 48], F32)
    mod5 = A.alloc([L, 6, 8, 3], F32)
    gsc = A.alloc([L, 2, 8, 3], F32)
    gn = A.alloc([L, 2, 8, 3], F32)
    wdt = A.alloc([L, 8, NH], BF16)
    wsT = A.alloc([L, 4, 128], BF16)
    cT = A.alloc([8, 3], F32)
    cTs = A.alloc([8, 3], BF16)
    lpb = A.alloc([4, 512], F32, name="lp")
    hT = A.alloc([L, 512], F32)
    hTb = A.alloc([L, 512], BF16)
    ctail = A.alloc([L, 8, 3], F32)
    xTs = [A.alloc([8, T], F32) for _ in range(2)]
    xT_smp = A.alloc([8, DSQ], F32)
    xtok_in = A.alloc([NS_MAX, D], F32, name="xtok_in")
    hTt_m = A.alloc([8, T], BF16)
    sq_m = A.alloc([8, T], BF16)
    rstd_m = A.alloc([T], F32)
    xbc_pre = A.alloc([8, T + 3], F32, name="xbc_pre")
    mT_m = A.alloc([8, T], F32, at=A.marks["xbc_pre"][0])
    vg = A.alloc([NS_MAX, 512], F32, at=A.marks["xbc_pre"][0])
    acc = [A.alloc([T], F32) for _ in range(3)]
    th = [A.alloc([T], F32) for _ in range(3)]
    xbcT = A.alloc([8, T], BF16)
    sz = A.alloc([NS_MAX, 512], BF16)
    vn = A.alloc([NS_MAX, 512], BF16)
    uT = A.alloc([4, T], BF16)
    catT = A.alloc([8, T], BF16)
    hTt_f = A.alloc([8, T], BF16)
    sq_f = A.alloc([8, T], BF16)
    rstd_f = A.alloc([T], F32)
    mT_f = A.alloc([8, T], F32, name="mT_f")
    xtok_out = A.alloc([NS_MAX, D], F32, at=A.marks["mT_f"][0])
    hid = A.alloc([32, T], BF16, name="hid")
    wmb = [A.alloc([8, 512], BF16, at=A.marks["hid"][0] + i * 8192) for i in range(2)]
    rl = [A.alloc([T], F32) for _ in range(3)]
    vst = A.alloc([NS_MAX, 6], F32)
    vmv = A.alloc([NS_MAX, 2], F32)
    vrs = A.alloc([NS_MAX], F32)
    dtraw = A.alloc([NS_MAX, NH], F32)
    dtt = A.alloc([NS_MAX, NH], F32)
    na = A.alloc([NS_MAX, NH], F32)
    na_hi = A.alloc([NS_MAX, NH], BF16)
    na_lo = A.alloc([NS_MAX, NH], BF16)
    nacum = A.alloc([NH], F32)
    dif = A.alloc([NH], F32)
    Ee = A.alloc([NH], F32)
    eend = A.alloc([NH], F32)
    dec = A.alloc([NH], F32)
    xdt = A.alloc([512], BF16)
    xD = A.alloc([512], F32)
    xw = A.alloc([512], BF16)
    btok = A.alloc([256], BF16)
    r1h = A.alloc([NS_MAX, NH, 128], BF16)
    r1l = A.alloc([NS_MAX, NH, 128], BF16)
    lexp = A.alloc([NH, 128], BF16)
    MT = A.alloc([NH, 128], BF16)
    cbm = A.alloc([2, 128], BF16)
    t1 = A.alloc([512], F32)
    yz = A.alloc([512], F32)
    junk, tz, vnf, vtmp = t1, t1, t1, yz
    ssq = A.alloc([2], F32)
    yn = A.alloc([512], BF16)
    stg = A.alloc([4, 128], F32)
    NRING = (A.nb - A.off - 64) // 8192
    assert NRING >= 6, NRING
    NRM = 3
    NRF = min(NRING - NRM, 4)
    ringM = [A.alloc([4096], BF16) for _ in range(NRM)]
    ringF = [A.alloc([4096], BF16) for _ in range(NRF)]
    rm_i, rf_i, pm_i, pf_i = [0], [0], [0], [0]

    def wloadM(src_blk):
        buf = ringM[rm_i[0] % NRM]
        rm_i[0] += 1
        dma("sp", buf, src_blk)
        return buf

    def wloadF(src_blk):
        buf = ringF[rf_i[0] % NRF]
        rf_i[0] += 1
        dma("sp", buf, src_blk)
        return buf

    def psM(dt=F32):
        b = banks[pm_i[0] % 6]
        pm_i[0] += 1
        return b.bitcast(dt)

    def psF(dt=F32):
        b = banks[6 + pf_i[0] % 2]
        pf_i[0] += 1
        return b.bitcast(dt)

    ps = psM

    dma("sp", cf, consts)
    for i, v in enumerate((RMS_EPS, 4 * RMS_EPS, LN_EPS, 1.0)):
        mset("pool", epsb[:, i:i + 1], v)
    cp("dve", cb[:, 0:4, :], cf.rearrange("p (a b) -> p a b", a=4))
    ts("dve", ntri_b, tri_f, -1.0, None, ALU.mult)
    eps_rms, eps_rms4, eps_ln, one_c = epsb[:, 0:1], epsb[:, 1:2], epsb[:, 2:3], epsb[:, 3:4]

    def cast_gen(l):
        for b, c0 in enumerate((0, 512, 1024, 1544, 2056)):
            dma("pool", s_in[l][b].rearrange("p (k n) -> p k n", k=8),
                w_in[l][:, c0:c0 + 512].rearrange("(k p) n -> p k n", p=128))
            yield
        for b in range(2):
            dma("pool", s_out[l][b].rearrange("p (k n) -> p k n", k=8),
                w_out[l][:, b * 512:(b + 1) * 512].rearrange("(k p) n -> p k n", p=128))
            yield
        for b in range(8):
            dma("pool", s_f1[l][b].rearrange("p (k n) -> p k n", k=8),
                w_ff1[l][:, b * 512:(b + 1) * 512].rearrange("(k p) n -> p k n", p=128))
            yield
        for b in range(8):
            for k4 in range(4):
                dma("pool", s_f2[l][b].rearrange("p (k n) -> p k n", k=32)[:, k4 * 8:(k4 + 1) * 8, :],
                    w_ff2[l][k4 * 1024:(k4 + 1) * 1024, b * 128:(b + 1) * 128].rearrange("(k p) n -> p k n", p=128))
                yield

    def cast_weights(l):
        for _ in cast_gen(l):
            pass

    cast_weights(0)
    cast_q = []

    for l in range(L):
        dma("sp", dtb_bc[:, l, :], bc_row(dt_bias[l], NH))
        dma("sp", apos_bc[:, l, :], bc_row(a_log[l], NH))
        dma("sp", dsk_bc[:, l, :], bc_row(d_skip[l], NH))
        for k in range(4):
            dma("sp", convw[:, l, :, k], conv_w[l, k].rearrange("(m p) -> p m", p=128), slow=True)
        dma("sp", convb[:, l, :], conv_b[l].rearrange("(m p) -> p m", p=128), slow=True)
        for k in range(4):
            dma("sp", ng[:, l, k, :], norm_g[l, k].rearrange("(m p) -> p m", p=128), slow=True)
        dma("sp", bmod[:, l, :], b_mod[l].rearrange("(m p) -> p m", p=128), slow=True)
        dma("pool", wdt[:, l, :, :], w_in[l][:, 1536:1544].rearrange("(k p) n -> p k n", p=128))
    for q in range(NPS + 1):
        dma("sp", cT[:, :, q], cvec[q].rearrange("(m p) -> p m", p=128), slow=True)
    act(apos_bc, apos_bc, AF.Exp)
    ts("dve", convw, convw, 0.5, None, ALU.mult)
    ts("dve", convb, convb, 0.5, None, ALU.mult)
    act(cTs, cT, AF.Silu)
    for l in range(L):
        for g in range(4):
            dma("sp", stg[:, g, :], w_s[l, g])
        p = ps()
        for g in range(4):
            tr(p[:, g * 128:(g + 1) * 128], stg[:, g, :], ident_f)
        tt("dve", wsT[:, l, :, :], p[:, 0:512].rearrange("p (g i) -> p g i", g=4),
           tri_f.rearrange("p (o i) -> p o i", o=1).to_broadcast([128, 4, 128]), ALU.mult)
    for l in range(L):
        pm = ps()
        for jb in range(12):
            wb_ = wmb[jb % 2]
            dma("pool", wb_, w_mod[l][:, jb * 512:(jb + 1) * 512].rearrange("(k p) n -> p k n", p=128))
            for j in range(4):
                jj = jb * 4 + j
                for kc in range(KC):
                    mm(pm[:, jj * 3:(jj + 1) * 3], wb_[:, kc, j * 128:(j + 1) * 128], cTs[:, kc, :],
                       start=(kc == 0), stop=(kc == KC - 1))
        tt("dve", mod5[:, l].rearrange("p w k q -> p (w k) q"), pm[:, 0:144].rearrange("p (j q) -> p j q", q=3),
           bmod[:, l, :].rearrange("p (j o) -> p j o", o=1).to_broadcast([128, 48, 3]), ALU.add)
        for i, (w_sc, n_i) in enumerate(((1, 0), (4, 2))):
            stt("dve", gsc[:, l, i], mod5[:, l, w_sc], 1.0,
                ng[:, l, n_i, :].rearrange("p (k o) -> p k o", o=1).to_broadcast([128, 8, 3]), ALU.add, ALU.mult)
        for i, (w_g, n_i) in enumerate(((2, 1), (5, 3))):
            tt("dve", gn[:, l, i], mod5[:, l, w_g],
               ng[:, l, n_i, :].rearrange("p (k o) -> p k o", o=1).to_broadcast([128, 8, 3]), ALU.mult)

    def stats_rstd(ps, src_sq, rstd, TT, eps_ap, n):
        p = ps()
        for kc in range(KC):
            mm(p[:, :TT], ones_b, src_sq[:, kc, :TT], start=(kc == 0), stop=(kc == KC - 1))
        act(rstd[:, :TT], p[:, :TT], AF.Ln, bias=eps_ap, scale=1.0 / n)
        act(rstd[:, :TT], rstd[:, :TT], AF.Exp, scale=-0.5)

    def bcast_t(v, TT):
        return v[:, :TT].rearrange("p (o t) -> p o t", o=1).to_broadcast([128, 8, TT])

    def prenorm(ps, xT, hTt, sq, rstd, tmpA, l, q, which, TT):
        act(sq[:, :, :TT], xT[:, :, :TT], AF.Square)
        stats_rstd(ps, sq, rstd, TT, eps_rms, D)
        yield
        tt("dve", tmpA[:, :, :TT], xT[:, :, :TT], bcast_t(rstd, TT), ALU.mult)
        shw = 0 if which == 0 else 3
        for kc in range(KC):
            ts("pool", hTt[:, kc, :TT], tmpA[:, kc, :TT], gsc[:, l, which, kc, q:q + 1],
               mod5[:, l, shw, kc, q:q + 1], ALU.mult, ALU.add)
        yield

    def postnorm_residual(ps, xT, mT, sq, rstd, l, q, which, TT):
        stats_rstd(ps, sq, rstd, TT, eps_rms, D)
        yield
        tt("dve", mT[:, :, :TT], mT[:, :, :TT], bcast_t(rstd, TT), ALU.mult)
        for kc in range(KC):
            stt("dve", xT[:, kc, :TT], mT[:, kc, :TT], gn[:, l, which, kc, q:q + 1], xT[:, kc, :TT], ALU.mult, ALU.add)
        yield

    class Tile:
        pass

    def Mgen(tl, l):
        q, TT, Sb, NS, is_sample, xT = tl.q, tl.TT, tl.Sb, tl.NS, tl.is_sample, tl.xT
        ps, wload = psM, wloadM
        if l == 0:
            if tl.first:
                if is_sample:
                    for l2 in range(L):
                        for k in range(3):
                            dma("sp", ctail[:, l2, :, k], state_conv[l2, 0, k].rearrange("(m p) -> p m", p=128), slow=True)
                        dma("sp", stg[:, :, :], state_ssd[l2, 0].rearrange("(c p) n -> p c n", p=128))
                        p = ps()
                        for c in range(4):
                            tr(p[:, c * 128:(c + 1) * 128], stg[:, c, :], ident_f)
                        cp("act", hT[:, l2, :], p[:, 0:512])
                        cp("pool", hTb[:, l2, :], hT[:, l2, :])
                else:
                    mset("pool", hT[:, :, :], 0.0)
                    mset("pool", hTb[:, :, :], 0.0)
                    mset("pool", ctail[:, :, :, :], 0.0)
            dma("sp", xtok_in[:Sb, :NS, :], tl.xin[tl.t0:tl.t0 + TT, :].rearrange("(s p) d -> p s d", p=Sb))
            yield
            for kc in range(KC):
                p = ps()
                for s in range(NS):
                    tr(p[:, s * Sb:(s + 1) * Sb], xtok_in[:Sb, s, kc * 128:(kc + 1) * 128], ident_f[:Sb, :Sb])
                cp("act" if kc % 2 == 0 else "dve", xT[:, kc, :TT], p[:, :TT])
                if kc % 2 == 1:
                    yield
        dma("sp", lpb[:, 0, :], bc_row(ssd_norm_g[l], 512))
        dma("sp", lpb[:, 1, :], bc_row(v_ln_g[l], 512))
        dma("sp", lpb[:, 2, :], bc_row(v_ln_b[l], 512))
        dma("sp", lpb[0:1, 3, :], b_s[l].rearrange("(o n) -> o n", o=1))
        ssdg_bc, vg_bc, vb_bc, bsrow = lpb[:, 0, :], lpb[:, 1, :], lpb[:, 2, :], lpb[0:1, 3, :]
        for _ in prenorm(ps, xT, hTt_m, sq_m, rstd_m, xbc_pre[:, :, 0:T], l, q, 0, TT):
            yield 2
        hTt = hTt_m
        W = wload(s_in[l][0]).rearrange("p (k n) -> p k n", k=8)
        for s in range(NS):
            p = ps()
            for kc in range(KC):
                mm(p[:Sb, :512], hTt[:, kc, s * Sb:(s + 1) * Sb], W[:, kc, :], start=(kc == 0), stop=(kc == KC - 1))
            act(tz[:Sb, :], p[:Sb, :512], AF.Tanh, scale=0.5)
            stt("dve", sz[:Sb, s, :], tz[:Sb, :], 1.0, p[:Sb, :512], ALU.add, ALU.mult)
            yield
        cp("pool", xbc_pre[:, :, 0:3], ctail[:, l, :, :])

        def conv_fin(m_):
            act(th[m_ % 3][:, :TT], acc[m_ % 3][:, :TT], AF.Tanh)
            tt("pool", th[m_ % 3][:, :TT], th[m_ % 3][:, :TT], acc[m_ % 3][:, :TT], ALU.mult)
            tt("pool", xbcT[:, m_, :TT], th[m_ % 3][:, :TT], acc[m_ % 3][:, :TT], ALU.add)

        Wx = wload(s_in[l][1]).rearrange("p (k n) -> p k n", k=8)
        Wb = wload(s_in[l][2]).rearrange("p (k n) -> p k n", k=8)
        for m in range(8):
            Wm = Wx if m < 4 else Wb
            p = ps()
            for kc in range(KC):
                mm(p[:, :TT], Wm[:, kc, (m % 4) * 128:(m % 4 + 1) * 128], hTt[:, kc, :TT],
                   start=(kc == 0), stop=(kc == KC - 1))
            cp("act", xbc_pre[:, m, 3:3 + TT], p[:, :TT])
            a_ = acc[m % 3]
            act(a_[:, :TT], p[:, :TT], AF.Identity, bias=convb[:, l, m:m + 1], scale=convw[:, l, m, 3:4])
            if m >= 2:
                conv_fin(m - 2)
            for k in range(0, 3):
                stt("dve", a_[:, :TT], xbc_pre[:, m, k:k + TT], convw[:, l, m, k:k + 1], a_[:, :TT], ALU.mult, ALU.add)
            yield m % 2
        conv_fin(6)
        conv_fin(7)
        cp("pool", ctail[:, l, :, :], xbc_pre[:, :, TT:TT + 3])
        if tl.last:
            for k in range(3):
                dma("sp", tl.conv_out[l, k].rearrange("(m p) -> p m", p=128), ctail[:, l, :, k], slow=True, is_out=True)
        Wu = wload(s_in[l][3]).rearrange("p (k n) -> p k n", k=8)
        for g in range(4):
            p = ps()
            for kc in range(KC):
                mm(p[:, :TT], Wu[:, kc, g * 128:(g + 1) * 128], hTt[:, kc, :TT], start=(kc == 0), stop=(kc == KC - 1))
            act(uT[:, g, :TT], p[:, :TT], AF.Gelu_apprx_tanh)
            if g % 2 == 1:
                yield
        Wv = wload(s_in[l][4]).rearrange("p (k n) -> p k n", k=8)
        for s in range(NS):
            p = ps()
            for kc in range(KC):
                mm(p[:Sb, :512], hTt[:, kc, s * Sb:(s + 1) * Sb], Wv[:, kc, :], start=(kc == 0), stop=(kc == KC - 1))
            act(vg[:Sb, s, :], p[:Sb, :512], AF.Gelu_apprx_tanh)
            S.add("dve", (lambda s=s: nc.vector.bn_stats(out=vst[:Sb, s, :], in_=vg[:Sb, s, :])),
                  reads=[vg[:Sb, s, :]], writes=[vst[:Sb, s, :]])
            S.add("dve", (lambda s=s: nc.vector.bn_aggr(out=vmv[:Sb, s, :], in_=vst[:Sb, s, :])),
                  reads=[vst[:Sb, s, :]], writes=[vmv[:Sb, s, :]])
            yield
        pd = ps()
        for s in range(NS):
            for kc in range(KC):
                mm(pd[:Sb, s * NH:(s + 1) * NH], hTt[:, kc, s * Sb:(s + 1) * Sb], wdt[:, l, kc, :],
                   start=(kc == 0), stop=(kc == KC - 1))
        tt("dve", dtraw[:Sb, :NS, :], pd[:Sb, :NS * NH].rearrange("p (s h) -> p s h", h=NH),
           dtb_bc[:Sb, l, :].rearrange("p (o h) -> p o h", o=1).to_broadcast([Sb, NS, NH]), ALU.add)
        yield
        act(dtt[:Sb, :NS, :], dtraw[:Sb, :NS, :], AF.Exp)
        act(dtt[:Sb, :NS, :], dtt[:Sb, :NS, :], AF.Ln, bias=one_c[:Sb, :])
        tt("dve", na[:Sb, :NS, :], dtt[:Sb, :NS, :],
           apos_bc[:Sb, l, :].rearrange("p (o h) -> p o h", o=1).to_broadcast([Sb, NS, NH]), ALU.mult)
        cp("pool", na_hi[:Sb, :NS, :], na[:Sb, :NS, :])
        tt("dve", na_lo[:Sb, :NS, :], na[:Sb, :NS, :], na_hi[:Sb, :NS, :], ALU.subtract)
        for s in range(NS):
            for (r1, src) in ((r1h, na_hi), (r1l, na_lo)):
                tt("dve", r1[:Sb, s, :, :Sb], src[:Sb, s, :].rearrange("p (h o) -> p h o", o=1).to_broadcast([Sb, NH, Sb]),
                   ntri_b[:Sb, :Sb].rearrange("p (o i) -> p o i", o=1).to_broadcast([Sb, NH, Sb]), ALU.mult)
        act(vrs[:Sb, :NS], vmv[:Sb, :NS, 1], AF.Ln, bias=eps_ln[:Sb, :])
        act(vrs[:Sb, :NS], vrs[:Sb, :NS], AF.Exp, scale=-0.5)
        yield
        for s in range(NS):
            ts("dve", vtmp[:Sb, :], vg[:Sb, s, :], vmv[:Sb, s, 0:1], vrs[:Sb, s:s + 1], ALU.subtract, ALU.mult)
            tt("pool", vtmp[:Sb, :], vtmp[:Sb, :], vg_bc[:Sb, :], ALU.mult)
            if is_sample:
                tt("pool", vnf[:Sb, :], vtmp[:Sb, :], vb_bc[:Sb, :], ALU.add)
                cp("pool", vn[:Sb, s, :], vnf[:Sb, :])
                dma("sp", v_sample[l, 0, :, :], vnf[:Sb, :], is_out=True)
            else:
                tt("pool", vn[:Sb, s, :], vtmp[:Sb, :], vb_bc[:Sb, :], ALU.add)
        yield
        for s in range(NS):
            tok = slice(s * Sb, (s + 1) * Sb)
            pT = ps(BF16)
            for c in range(6):
                tr(pT[:Sb, c * 128:(c + 1) * 128], xbcT[:, c, tok], ident_b)
            pc = ps()
            mm(pc[:Sb, 0:NH], tri_f[:Sb, :Sb], na[:Sb, s, :])
            mm(pc[:, NH:2 * NH], ones_f[:Sb, :], na[:Sb, s, :])
            segs = []
            for hf in range(2):
                pg = ps()
                o = pg[:Sb, :4 * Sb]
                hs = slice(4 * hf, 4 * hf + 4)
                mm(o, ones_b[:Sb, :Sb], r1h[:Sb, s, hs, :Sb], start=True, stop=False)
                mm(o, ones_b[:Sb, :Sb], r1l[:Sb, s, hs, :Sb], start=False, stop=False)
                mm(o, tri_b[:Sb, :Sb], na_hi[:Sb, s, hs].rearrange("p (h o) -> p h o", o=1).to_broadcast([Sb, 4, Sb]),
                   start=False, stop=False)
                mm(o, tri_b[:Sb, :Sb], na_lo[:Sb, s, hs].rearrange("p (h o) -> p h o", o=1).to_broadcast([Sb, 4, Sb]),
                   start=False, stop=False)
                mm(o, ident_b[:Sb, :Sb], negm_b[:Sb, :Sb].rearrange("p (o i) -> p o i", o=1).to_broadcast([Sb, 4, Sb]),
                   start=False, stop=True)
                segs.append(pg)
            pcb = ps()
            for g in range(2):
                mm(pcb[:Sb, g * Sb:(g + 1) * Sb], xbcT[:, 4 + g, tok], xbcT[:, 6 + g, tok])
            yield 2
            tt("dve", xdt[:Sb, :].rearrange("p (h d) -> p h d", h=NH), pT[:Sb, 0:512].rearrange("p (h d) -> p h d", h=NH),
               dtt[:Sb, s, :].rearrange("p (h o) -> p h o", o=1).to_broadcast([Sb, NH, HD]), ALU.mult)
            tt("dve", cbm[:Sb, :, :Sb], pcb[:Sb, :2 * Sb].rearrange("p (g i) -> p g i", g=2),
               tri_f[:Sb, :Sb].rearrange("p (o i) -> p o i", o=1).to_broadcast([Sb, 2, Sb]), ALU.mult)
            for hf in range(2):
                act(lexp[:Sb, 4 * hf:4 * hf + 4, :Sb], segs[hf][:Sb, :4 * Sb].rearrange("p (h i) -> p h i", h=4), AF.Exp)
                tt("dve", MT[:Sb, 4 * hf:4 * hf + 4, :Sb], lexp[:Sb, 4 * hf:4 * hf + 4, :Sb],
                   cbm[:Sb, hf, :Sb].rearrange("p (o i) -> p o i", o=1).to_broadcast([Sb, 4, Sb]), ALU.mult)
            cp("act", nacum[:Sb, :], pc[:Sb, 0:NH])
            act(dec[:, :], pc[:, NH:2 * NH], AF.Exp, scale=-1.0)
            tt("dve", dif[:Sb, :], pc[:Sb, NH:2 * NH], nacum[:Sb, :], ALU.subtract)
            act(Ee[:Sb, :], nacum[:Sb, :], AF.Exp, scale=-1.0)
            act(eend[:Sb, :], dif[:Sb, :], AF.Exp, scale=-1.0)
            cp("act", btok[:Sb, :], pT[:Sb, 512:768])
            tt("dve", xD[:Sb, :].rearrange("p (h d) -> p h d", h=NH), pT[:Sb, 0:512].rearrange("p (h d) -> p h d", h=NH),
               dsk_bc[:Sb, l, :].rearrange("p (h o) -> p h o", o=1).to_broadcast([Sb, NH, HD]), ALU.mult)
            yield 1
            py = ps()
            for h in range(NH):
                mm(py[:Sb, h * HD:(h + 1) * HD], MT[:Sb, h, :Sb], xdt[:Sb, h * HD:(h + 1) * HD])
            pyo = ps()
            for g in range(2):
                mm(pyo[:Sb, g * 256:(g + 1) * 256], xbcT[:, 6 + g, tok], hTb[:, l, g * 256:(g + 1) * 256])
            tt("pool", xw[:Sb, :].rearrange("p (h d) -> p h d", h=NH), xdt[:Sb, :].rearrange("p (h d) -> p h d", h=NH),
               eend[:Sb, :].rearrange("p (h o) -> p h o", o=1).to_broadcast([Sb, NH, HD]), ALU.mult)
            pst = ps()
            for g in range(2):
                mm(pst[:, g * 256:(g + 1) * 256], btok[:Sb, g * 128:(g + 1) * 128], xw[:Sb, g * 256:(g + 1) * 256])
            pmx = ps()
            for g in range(4):
                mm(pmx[:, g * Sb:(g + 1) * Sb], vn[:Sb, s, g * 128:(g + 1) * 128], wsT[:Sb, l, g, :Sb], start=True, stop=False)
                mm(pmx[:, g * Sb:(g + 1) * Sb], ones_f[0:1, :], bsrow[0:1, g * 128:g * 128 + Sb], start=False, stop=True)
            yield 2
            tt("dve", t1[:Sb, :].rearrange("p (h d) -> p h d", h=NH), pyo[:Sb, 0:512].rearrange("p (h d) -> p h d", h=NH),
               Ee[:Sb, :].rearrange("p (h o) -> p h o", o=1).to_broadcast([Sb, NH, HD]), ALU.mult)
            tt("dve", t1[:Sb, :], py[:Sb, 0:512], t1[:Sb, :], ALU.add)
            tt("pool", t1[:Sb, :], t1[:Sb, :], xD[:Sb, :], ALU.add)
            yield 1
            tt("pool", yz[:Sb, :], t1[:Sb, :], sz[:Sb, s, :], ALU.mult)
            yield 1
            act(junk[:Sb, :], yz[:Sb, :], AF.Square, accum=ssq[:Sb, 0:1])
            act(ssq[:Sb, 1:2], ssq[:Sb, 0:1], AF.Ln, bias=eps_rms4[:Sb, :], scale=1.0 / SSDW)
            act(ssq[:Sb, 1:2], ssq[:Sb, 1:2], AF.Exp, scale=-0.5)
            yield 1
            stt("dve", yn[:Sb, :], yz[:Sb, :], ssq[:Sb, 1:2], ssdg_bc[:Sb, :], ALU.mult, ALU.mult)
            tt("dve", catT[:, 4:8, tok], pmx[:, :4 * Sb].rearrange("p (g i) -> p g i", g=4), uT[:, :, tok], ALU.mult)
            tt("dve", hT[:, l, :].rearrange("p (h d) -> p h d", h=NH), hT[:, l, :].rearrange("p (h d) -> p h d", h=NH),
               dec[:, :].rearrange("p (h o) -> p h o", o=1).to_broadcast([128, NH, HD]), ALU.mult)
            tt("dve", hT[:, l, :], pst[:, 0:512], hT[:, l, :], ALU.add)
            cp("pool", hTb[:, l, :], hT[:, l, :])
            yield 2
            pT2 = ps(BF16)
            for c in range(4):
                tr(pT2[:, c * Sb:(c + 1) * Sb], yn[:Sb, c * 128:(c + 1) * 128], ident_b[:Sb, :Sb])
            cp("act", catT[:, 0:4, tok], pT2[:, :4 * Sb].rearrange("p (c i) -> p c i", c=4))
            yield 1
        if tl.last:
            p = ps()
            for c in range(4):
                tr(p[:, c * 128:(c + 1) * 128], hT[:, l, c * 128:(c + 1) * 128], ident_f)
            cp("act", stg[:, :, :], p[:, 0:512].rearrange("p (c n) -> p c n", c=4))
            dma("sp", tl.ssd_out[l].rearrange("(c p) n -> p c n", p=128), stg[:, :, :], is_out=True)
        for b in range(2):
            Wo = wload(s_out[l][b]).rearrange("p (k n) -> p k n", k=8)
            for mi in range(4):
                m = b * 4 + mi
                p = ps()
                for kc in range(KC):
                    mm(p[:, :TT], Wo[:, kc, mi * 128:(mi + 1) * 128], catT[:, kc, :TT], start=(kc == 0), stop=(kc == KC - 1))
                cp("act", mT_m[:, m, :TT], p[:, :TT])
                act(sq_m[:, m, :TT], p[:, :TT], AF.Square)
                if mi % 2 == 1:
                    yield
        for _ in postnorm_residual(ps, xT, mT_m, sq_m, rstd_m, l, q, 0, TT):
            yield 3
        for _ in prenorm(ps, xT, hTt_f, sq_m, rstd_m, mT_m, l, q, 1, TT):
            yield 3

    def Fgen(tl, l):
        q, TT, Sb, NS, xT = tl.q, tl.TT, tl.Sb, tl.NS, tl.xT
        ps, wload = psF, wloadF
        for b in range(8):
            W1 = wload(s_f1[l][b]).rearrange("p (k n) -> p k n", k=8)
            for hi in range(4):
                hc = b * 4 + hi
                p = ps()
                for kc in range(KC):
                    mm(p[:, :TT], W1[:, kc, hi * 128:(hi + 1) * 128], hTt_f[:, kc, :TT], start=(kc == 0), stop=(kc == KC - 1))
                r_ = rl[hc % 3]
                act(r_[:, :TT], p[:, :TT], AF.Relu)
                tt("pool", hid[:, hc, :TT], r_[:, :TT], r_[:, :TT], ALU.mult)
                yield
        for m in range(8):
            W2 = wload(s_f2[l][m]).rearrange("p (k n) -> p k n", k=32)
            p = ps()
            for kc in range(32):
                mm(p[:, :TT], W2[:, kc, :], hid[:, kc, :TT], start=(kc == 0), stop=(kc == 31))
                if kc % 8 == 7 and kc != 31:
                    yield
            cp("act", mT_f[:, m, :TT], p[:, :TT])
            act(sq_f[:, m, :TT], p[:, :TT], AF.Square)
            yield
        yield from postnorm_residual(ps, xT, mT_f, sq_f, rstd_f, l, q, 1, TT)
        if l == L - 1:
            for s in range(NS):
                for half in range(2):
                    p = ps()
                    for k4 in range(4):
                        kc = half * 4 + k4
                        tr(p[:Sb, k4 * 128:(k4 + 1) * 128], xT[:, kc, s * Sb:(s + 1) * Sb], ident_f)
                    cp("act" if half == 0 else "dve", xtok_out[:Sb, s, half * 512:(half + 1) * 512], p[:Sb, 0:512])
                yield
            dma("sp", tl.yout[tl.t0:tl.t0 + TT, :].rearrange("(s p) d -> p s d", p=Sb), xtok_out[:Sb, :NS, :], is_out=True)
            yield

    tiles = []
    streams = [("p", i) for i in range(NPS)] + [("s", 0)]
    for q, (kind, bi) in enumerate(streams):
        if kind == "s":
            Ltok, TT, Sb, NS = DSQ, DSQ, DSQ, 1
            xin, yout, conv_out, ssd_out = x_sample[0], y_sample[0], conv_sample[:, 0], ssd_sample[:, 0]
        else:
            Ltok, TT, Sb, NS = SEQ, T, 128, T // 128
            xin, yout, conv_out, ssd_out = x_prompt[bi], y_prompt[bi], conv_prompt[:, bi], ssd_prompt[:, bi]
        nt = Ltok // TT
        for ti in range(nt):
            tl = Tile()
            tl.q, tl.TT, tl.Sb, tl.NS, tl.is_sample = q, TT, Sb, NS, kind == "s"
            tl.xin, tl.yout, tl.conv_out, tl.ssd_out = xin, yout, conv_out, ssd_out
            tl.t0, tl.first, tl.last = ti * TT, ti == 0, ti == nt - 1
            tl.xT = xT_smp if kind == "s" else xTs[len(tiles) % 2]
            tiles.append(tl)

    def drive(gens):
        gf, gm = gens if len(gens) == 2 else (gens[0], None)
        f_done = gf is None

        def fstep():
            nonlocal f_done
            if not f_done:
                try:
                    next(gf)
                except StopIteration:
                    f_done = True

        if gm is not None:
            for _ in range(10):
                fstep()
            for k in gm:
                if cast_q:
                    try:
                        next(cast_q[0])
                    except StopIteration:
                        cast_q.pop(0)
                for _ in range(1 if k is None else k):
                    fstep()
        while not f_done:
            fstep()

    order = []
    groups = [tiles[g0:g0 + 2] for g0 in range(0, len(tiles), 2)]
    for grp in groups:
        for l in range(L):
            for tl in grp:
                order.append((tl, l))
    pending = None
    cast_done = {0}
    for (tl, l) in order:
        if tl is tiles[0] and l + 1 < L and (l + 1) not in cast_done:
            for g_ in cast_q:
                for _ in g_:
                    pass
            del cast_q[:]
            cast_q.append(cast_gen(l + 1))
            cast_done.add(l + 1)
        if pending is not None and pending[0] is tl:
            drive([Fgen(*pending)])
            pending = None
        drive([Fgen(*pending) if pending is not None else None, Mgen(tl, l)])
        pending = (tl, l)
    drive([Fgen(*pending)])

    sem_names = [("c", e) for e in ("pe", "act", "dve", "pool")]
    for qn, k in S.dma_slots.items():
        sem_names += [(qn, i) for i in range(k)]
    import contextlib
    with contextlib.ExitStack() as st:
        sems = {}
        for i, sn in enumerate(sem_names):
            sems[sn] = st.enter_context(nc.semaphore("sem%d" % i))
        block = st.enter_context(nc.Block())

        @block.sync
        def _(sync):
            S.emit(sems)
    return nc, S


_OUT_NAMES = ["y_prompt", "y_sample", "conv_prompt", "ssd_prompt", "conv_sample", "ssd_sample", "v_sample"]


def run(cfg, inputs, trace=False):
    L, NPS, NCO = cfg.depth, cfg.nps, cfg.ncores
    nc, S = build(cfg)
    f = lambda a: np.ascontiguousarray(np.asarray(a, dtype=np.float32))
    shared = {k: f(inputs[k])[:L] for k in ("w_mod", "b_mod", "norm_g", "w_in", "conv_w", "conv_b", "dt_bias", "a_log",
                                             "d_skip", "ssd_norm_g", "v_ln_g", "v_ln_b", "w_s", "w_out", "w_ff1", "w_ff2")}
    shared["b_s"] = f(inputs["b_s"])[:L].reshape(L, 512)
    shared["consts"] = consts_host()
    xp, xs = f(inputs["x_prompt"]), f(inputs["x_sample"])
    sc, ss = f(inputs["state_conv"]), f(inputs["state_ssd"])
    cpv, csv = f(inputs["c_prompt"]), f(inputs["c_sample"])
    in_maps = []
    for c in range(NCO):
        m = dict(shared)
        m["x_prompt"] = xp[c * NPS:(c + 1) * NPS, :cfg.seq]
        m["x_sample"] = xs[c:c + 1, :cfg.dseq]
        m["state_conv"] = np.ascontiguousarray(sc[:L, c:c + 1])
        m["state_ssd"] = np.ascontiguousarray(ss[:L, c:c + 1]).reshape(L, 1, NH * HD, NST)
        m["cvec"] = np.ascontiguousarray(np.concatenate([cpv[c * NPS:(c + 1) * NPS], csv[c:c + 1]], axis=0))
        in_maps.append(m)
    res = run_bass_kernel_spmd(nc, in_maps, core_ids=list(range(NCO)), trace=trace)
    R = res.results
    outs = (
        np.concatenate([r["y_prompt"] for r in R], axis=0),
        np.concatenate([r["y_sample"] for r in R], axis=0),
        np.concatenate([r["conv_prompt"] for r in R], axis=1),
        np.concatenate([r["ssd_prompt"] for r in R], axis=1).reshape(L, NCO * NPS, NH, HD, NST),
        np.concatenate([r["conv_sample"] for r in R], axis=1),
        np.concatenate([r["ssd_sample"] for r in R], axis=1).reshape(L, NCO, NH, HD, NST),
        np.concatenate([r["v_sample"] for r in R], axis=1),
    )
    return tuple(np.ascontiguousarray(o, dtype=np.float32) for o in outs), res


def kernel(**inputs):
    cfg = Cfg()
    outs, _ = run(cfg, inputs)
    return outs
```

```python
import numpy as np
import concourse.bass as bass
import concourse.mybir as mybir
from concourse.bass_utils import run_bass_kernel_spmd

F32 = mybir.dt.float32
BF16 = mybir.dt.bfloat16
U8 = mybir.dt.uint8
AF = mybir.ActivationFunctionType
ALU = mybir.AluOpType

D = 1024
KC = 8
SSDW = 512
CONVD = 1024
NH = 8
HD = 64
NST = 128
CW = 512
DFF = 4096
IN_DIM = 2568
NMOD = 6
RMS_EPS = 1e-6
LN_EPS = 1e-5
NEG = -30000.0


def dsize(dt):
    return mybir.dt.size(dt)


class Cfg:
    def __init__(self, depth=4, nps=2, seq=4096, dseq=16, ncores=8, T=256):
        self.depth, self.nps, self.seq, self.dseq, self.ncores, self.T = depth, nps, seq, dseq, ncores, T


class Op:
    __slots__ = ("eng", "fn", "deps", "is_dma", "sig", "sem", "val", "idx")


class Sched:
    ENG = ("pe", "act", "dve", "pool", "sp")

    def __init__(self, nc, tracked_dram=()):
        self.nc = nc
        self.ops = []
        self.regions = {}
        self.psum_last = {}
        self.tracked_dram = set(tracked_dram)
        self.dma_slots = {"sp": 16, "pool": 8, "act": 4}
        self.dma_count = {"sp": 0, "pool": 0, "act": 0}
        self.dma_slot_last = {}
        self.out_dmas = []

    def _rng(self, ap):
        sp = str(ap.space)
        name = ap.tensor.name
        if "DRAM" in sp:
            if name in self.tracked_dram:
                ext = 1
                for st_, c_ in ap.ap:
                    ext += (c_ - 1) * abs(st_)
                return ("d", name, ap.offset, ap.offset + ext, 0, 128)
            return None
        esz = dsize(ap.dtype)
        pairs = ap.ap
        pstep, pcnt = pairs[0]
        off = ap.offset
        if pstep > 0:
            p0 = off // pstep
            foff = off % pstep
        else:
            p0, foff = 0, off
        if "PSUM" in sp:
            return ("p", name, 0, 0, 0, 0)
        ext = 1
        for s, c in pairs[1:]:
            ext += (c - 1) * abs(s)
        lo = foff * esz
        return ("s", name, lo, lo + ext * esz, p0, p0 + pcnt)

    def add(self, eng, fn, reads=(), writes=(), is_dma=False, is_out=False):
        op = Op()
        op.eng, op.fn, op.is_dma, op.sig, op.sem, op.val = eng, fn, is_dma, False, None, 0
        op.idx = len(self.ops)
        deps = {}

        def dep(i, kind):
            if i is None or i == op.idx:
                return
            k = deps.get(i)
            if k is None or kind == "raw":
                deps[i] = kind

        for ap, is_w in [(a, False) for a in reads] + [(a, True) for a in writes]:
            r = self._rng(ap)
            if r is None:
                continue
            kind, name, lo, hi, plo, phi = r
            if kind == "p":
                last = self.psum_last.setdefault(name, {})
                for e, i in last.items():
                    if e == eng and eng == "pe":
                        continue
                    dep(i, "raw")
                continue
            lst = self.regions.setdefault(name, [])
            keep = []
            for ent in lst:
                elo, ehi, eplo, ephi, ei, ew, eeng, edma = ent
                ov = not (ehi <= lo or hi <= elo or ephi <= plo or phi <= eplo)
                if ov:
                    if is_w:
                        if not (kind == "d" and ew):
                            dep(ei, "waw" if ew else "war")
                    elif ew:
                        dep(ei, "raw")
                if kind != "d" and is_w and ov and elo >= lo and ehi <= hi and eplo >= plo and ephi <= phi:
                    continue
                if (not is_w) and (not ew) and eeng == eng and (not edma) and (not is_dma) \
                        and elo == lo and ehi == hi and eplo == plo and ephi == phi:
                    continue
                keep.append(ent)
            keep.append((lo, hi, plo, phi, op.idx, is_w, eng, is_dma))
            self.regions[name] = keep
        for ap in list(reads) + list(writes):
            r = self._rng(ap)
            if r is not None and r[0] == "p":
                self.psum_last[r[1]][eng] = op.idx
        if is_dma:
            n = self.dma_count[eng]
            self.dma_count[eng] = n + 1
            slot = (eng, n % self.dma_slots[eng])
            prev = self.dma_slot_last.get(slot)
            if prev is not None:
                deps[prev] = "slot"
            self.dma_slot_last[slot] = op.idx
            op.sem = slot
            op.sig = True
            if is_out:
                self.out_dmas.append(op.idx)
        best = {}
        red = {}
        for i, kind in deps.items():
            d = self.ops[i]
            if d.is_dma:
                red[i] = kind
                continue
            cur = best.get(d.eng)
            if cur is None:
                best[d.eng] = [i, kind]
            else:
                if i > cur[0]:
                    cur[0] = i
                if kind == "raw":
                    cur[1] = "raw"
        for e, (i, kind) in best.items():
            red[i] = kind
        op.deps = red
        self.ops.append(op)
        return op

    def emit(self, sems):
        nc = self.nc
        H = {"pe": nc.tensor, "act": nc.scalar, "dve": nc.vector, "pool": nc.gpsimd, "sp": nc.sync}
        ops = self.ops
        need = []
        for op in ops:
            w = []
            for i, kind in op.deps.items():
                d = ops[i]
                if d.is_dma:
                    w.append(i)
                elif op.is_dma:
                    w.append(i)
                elif d.eng != op.eng:
                    w.append(i)
                else:
                    if op.eng != "pe" and kind == "raw":
                        w.append(i)
            for i in w:
                ops[i].sig = True
            need.append(w)
        cnt = {e: 0 for e in self.ENG}
        slot_cnt = {}
        for op in ops:
            if not op.sig:
                continue
            if op.is_dma:
                c = slot_cnt.get(op.sem, 0) + 1
                slot_cnt[op.sem] = c
                op.val = 16 * c
            else:
                cnt[op.eng] += 1
                op.sem = ("c", op.eng)
                op.val = cnt[op.eng]
        waited = {e: {} for e in self.ENG}
        nwait = 0
        for op, w in zip(ops, need):
            h = H[op.eng]
            wd = waited[op.eng]
            tgt = {}
            for i in w:
                d = ops[i]
                if wd.get(d.sem, 0) < d.val and tgt.get(d.sem, 0) < d.val:
                    tgt[d.sem] = d.val
            for s, v in tgt.items():
                h.wait_ge(sems[s], v)
                wd[s] = v
                nwait += 1
            inst = op.fn()
            if op.sig:
                inst.then_inc(sems[op.sem], 16 if op.is_dma else 1)
        for s, c in slot_cnt.items():
            nc.sync.wait_ge(sems[s], 16 * c)
        for e in ("pe", "act", "dve", "pool"):
            if cnt[e] > 0:
                nc.sync.wait_ge(sems[("c", e)], cnt[e])
        self.stats = dict(nops=len(ops), nwait=nwait, cnt=cnt)


class Arena:
    def __init__(self, nc, nbytes):
        self.nb = nbytes
        self.ap = nc.alloc_sbuf_tensor("arena", [128, nbytes], U8).ap()
        self.off = 0
        self.marks = {}

    def alloc(self, shape, dtype, at=None, name=None):
        n = 1
        for s in shape:
            n *= s
        nb = n * dsize(dtype)
        if at is None:
            off = (self.off + 31) // 32 * 32
            self.off = off + nb
            assert self.off <= self.nb, ("SBUF arena overflow", name, self.off, self.nb)
        else:
            off = at
            assert off + nb <= self.nb
        if name:
            self.marks[name] = (off, nb)
        v = self.ap[:, off:off + nb].bitcast(dtype)
        if len(shape) == 2:
            v = v.rearrange("p (a b) -> p a b", a=shape[0])
        elif len(shape) == 3:
            v = v.rearrange("p (a b c) -> p a b c", a=shape[0], b=shape[1])
        elif len(shape) == 4:
            v = v.rearrange("p (a b c d) -> p a b c d", a=shape[0], b=shape[1], c=shape[2])
        return v


def consts_host():
    c = np.zeros((128, 512), np.float32)
    j = np.arange(128)[:, None]
    i = np.arange(128)[None, :]
    c[:, 0:128] = np.eye(128, dtype=np.float32)
    c[:, 128:256] = (j <= i).astype(np.float32)
    c[:, 256:384] = np.where(i >= j, 0.0, NEG)
    c[:, 384:512] = 1.0
    return c


def build(cfg):
    L, NPS, SEQ, DSQ, T = cfg.depth, cfg.nps, cfg.seq, cfg.dseq, cfg.T
    nc = bass.Bass("TRN2", target_bir_lowering=False)

    def din(name, shape, dt=F32):
        return nc.dram_tensor(name, list(shape), dt, kind="ExternalInput").ap()

    def dout(name, shape, dt=F32):
        return nc.dram_tensor(name, list(shape), dt, kind="ExternalOutput").ap()

    def dscr(name, shape, dt=BF16):
        return nc.dram_tensor(name, list(shape), dt, kind="Internal").ap()

    x_prompt = din("x_prompt", [NPS, SEQ, D])
    x_sample = din("x_sample", [1, DSQ, D])
    state_conv = din("state_conv", [L, 1, 3, CONVD])
    state_ssd = din("state_ssd", [L, 1, NH * HD, NST])
    cvec = din("cvec", [NPS + 1, D])
    consts = din("consts", [128, 512])
    w_mod = din("w_mod", [L, D, NMOD * D])
    b_mod = din("b_mod", [L, NMOD * D])
    norm_g = din("norm_g", [L, 4, D])
    w_in = din("w_in", [L, D, IN_DIM])
    conv_w = din("conv_w", [L, 4, CONVD])
    conv_b = din("conv_b", [L, CONVD])
    dt_bias = din("dt_bias", [L, NH])
    a_log = din("a_log", [L, NH])
    d_skip = din("d_skip", [L, NH])
    ssd_norm_g = din("ssd_norm_g", [L, SSDW])
    v_ln_g = din("v_ln_g", [L, CW])
    v_ln_b = din("v_ln_b", [L, CW])
    w_s = din("w_s", [L, 4, 128, 128])
    b_s = din("b_s", [L, 4 * 128])
    w_out = din("w_out", [L, D, D])
    w_ff1 = din("w_ff1", [L, D, DFF])
    w_ff2 = din("w_ff2", [L, DFF, D])

    y_prompt = dout("y_prompt", [NPS, SEQ, D])
    y_sample = dout("y_sample", [1, DSQ, D])
    conv_prompt = dout("conv_prompt", [L, NPS, 3, CONVD])
    ssd_prompt = dout("ssd_prompt", [L, NPS, NH * HD, NST])
    conv_sample = dout("conv_sample", [L, 1, 3, CONVD])
    ssd_sample = dout("ssd_sample", [L, 1, NH * HD, NST])
    v_sample = dout("v_sample", [L, 1, DSQ, CW])

    NBLK_IN, NBLK_OUT, NBLK_F1, NBLK_F2 = 5, 2, 8, 8
    BLK = 128 * 4096
    s_in = [dscr("s_in%d" % l, [NBLK_IN, 128, 4096]) for l in range(L)]
    s_out = [dscr("s_out%d" % l, [NBLK_OUT, 128, 4096]) for l in range(L)]
    s_f1 = [dscr("s_f1%d" % l, [NBLK_F1, 128, 4096]) for l in range(L)]
    s_f2 = [dscr("s_f2%d" % l, [NBLK_F2, 128, 4096]) for l in range(L)]
    tracked = [t.tensor.name for t in s_in + s_out + s_f1 + s_f2]

    S = Sched(nc, tracked_dram=tracked)
    A = Arena(nc, 206 * 1024)
    banks = [nc.alloc_psum_tensor("pb%d" % i, [128, 2048], U8).ap() for i in range(8)]
    bank_i = [0]

    def psum(dt=F32):
        b = banks[bank_i[0] % 8]
        bank_i[0] += 1
        return b.bitcast(dt)

    def mm(out, lhsT, rhs, start=True, stop=True):
        return S.add("pe", lambda: nc.tensor.matmul(out, lhsT=lhsT, rhs=rhs, start=start, stop=stop),
                     reads=[lhsT, rhs], writes=[out])

    def tr(out, in_, ident):
        return S.add("pe", lambda: nc.tensor.transpose(out, in_, ident), reads=[in_, ident], writes=[out])

    def act(out, in_, func, bias=None, scale=None, accum=None):
        kw = {}
        rd = [in_]
        wr = [out]
        if bias is not None:
            kw["bias"] = bias
            if not isinstance(bias, (int, float)):
                rd.append(bias)
        if scale is not None:
            kw["scale"] = scale
            if not isinstance(scale, (int, float)):
                rd.append(scale)
        if accum is not None:
            kw["accum_out"] = accum
            wr.append(accum)
        return S.add("act", lambda: nc.scalar.activation(out=out, in_=in_, func=func, **kw), reads=rd, writes=wr)

    def _eng(e):
        return nc.vector if e == "dve" else nc.gpsimd

    def tt(e, out, in0, in1, op):
        return S.add(e, lambda: _eng(e).tensor_tensor(out=out, in0=in0, in1=in1, op=op), reads=[in0, in1], writes=[out])

    def ts(e, out, in0, s1, s2, op0, op1=None):
        rd = [in0] + [s for s in (s1, s2) if s is not None and not isinstance(s, (int, float))]
        if op1 is None:
            return S.add(e, lambda: _eng(e).tensor_scalar(out=out, in0=in0, scalar1=s1, scalar2=None, op0=op0),
                         reads=rd, writes=[out])
        return S.add(e, lambda: _eng(e).tensor_scalar(out=out, in0=in0, scalar1=s1, scalar2=s2, op0=op0, op1=op1),
                     reads=rd, writes=[out])

    def stt(e, out, in0, scalar, in1, op0, op1):
        rd = [in0, in1] + ([scalar] if not isinstance(scalar, (int, float)) else [])
        return S.add(e, lambda: _eng(e).scalar_tensor_tensor(out=out, in0=in0, scalar=scalar, in1=in1, op0=op0, op1=op1),
                     reads=rd, writes=[out])

    def cp(e, out, in_):
        if e == "act":
            return act(out, in_, AF.Copy)
        return S.add(e, lambda: _eng(e).tensor_copy(out=out, in_=in_), reads=[in_], writes=[out])

    def mset(e, ap, val):
        return S.add(e, lambda: _eng(e).memset(ap, val), writes=[ap])

    def dma(q, out, in_, slow=False, is_out=False):
        h = {"sp": nc.sync, "pool": nc.gpsimd, "act": nc.scalar}[q]
        if slow:
            fn = lambda: h.dma_start(out=out, in_=in_, allow_slow_non_contiguous=True)
        else:
            fn = lambda: h.dma_start(out=out, in_=in_)
        return S.add(q, fn, reads=[in_], writes=[out], is_dma=True, is_out=is_out)

    def bc_row(dram_ap_row, n):
        return bass.AP(dram_ap_row.tensor, dram_ap_row.offset, [[0, 128], [1, n]])

    NS_MAX = max(T // 128, 1)
    cf = A.alloc([512], F32, name="cf")
    ident_f, tri_f, negm_f, ones_f = cf[:, 0:128], cf[:, 128:256], cf[:, 256:384], cf[:, 384:512]
    cb = A.alloc([5, 128], BF16, name="cb")
    ident_b, tri_b, negm_b, ones_b, ntri_b = cb[:, 0, :], cb[:, 1, :], cb[:, 2, :], cb[:, 3, :], cb[:, 4, :]
    epsb = A.alloc([4], F32)
    dtb_bc = A.alloc([L, NH], F32)
    apos_bc = A.alloc([L, NH], F32)
    dsk_bc = A.alloc([L, NH], F32)
    convw = A.alloc([L, 8, 4], F32)
    convb = A.alloc([L, 8], F32)
    ng = A.alloc([L, 4, 8], F32)
    bmod = A.alloc([L, 48], F32)
    mod5 = A.alloc([L, 6, 8, 3], F32)
    gsc = A.alloc([L, 2, 8, 3], F32)
    gn = A.alloc([L, 2, 8, 3], F32)
    wdt = A.alloc([L, 8, NH], BF16)
    wsT = A.alloc([L, 4, 128], BF16)
    cT = A.alloc([8, 3], F32)
    cTs = A.alloc([8, 3], BF16)
    lpb = A.alloc([4, 512], F32, name="lp")
    hT = A.alloc([L, 512], F32)
    hTb = A.alloc([L, 512], BF16)
    ctail = A.alloc([L, 8, 3], F32)
    xTs = [A.alloc([8, T], F32) for _ in range(2)]
    xT_smp = A.alloc([8, DSQ], F32)
    xtok_in = A.alloc([NS_MAX, D], F32, name="xtok_in")
    hTt_m = A.alloc([8, T], BF16)
    sq_m = A.alloc([8, T], BF16)
    rstd_m = A.alloc([T], F32)
    xbc_pre = A.alloc([8, T + 3], F32, name="xbc_pre")
    mT_m = A.alloc([8, T], F32, at=A.marks["xbc_pre"][0])
    vg = A.alloc([NS_MAX, 512], F32, at=A.marks["xbc_pre"][0])
    acc = [A.alloc([T], F32) for _ in range(3)]
    th = [A.alloc([T], F32) for _ in range(3)]
    xbcT = A.alloc([8, T], BF16)
    sz = A.alloc([NS_MAX, 512], BF16)
    vn = A.alloc([NS_MAX, 512], BF16)
    uT = A.alloc([4, T], BF16)
    catT = A.alloc([8, T], BF16)
    hTt_f = A.alloc([8, T], BF16)
    sq_f = A.alloc([8, T], BF16)
    rstd_f = A.alloc([T], F32)
    mT_f = A.alloc([8, T], F32, name="mT_f")
    xtok_out = A.alloc([NS_MAX, D], F32, at=A.marks["mT_f"][0])
    hid = A.alloc([32, T], BF16, name="hid")
    wmb = [A.alloc([8, 512], BF16, at=A.marks["hid"][0] + i * 8192) for i in range(2)]
    rl = [A.alloc([T], F32) for _ in range(3)]
    vst = A.alloc([NS_MAX, 6], F32)
    vmv = A.alloc([NS_MAX, 2], F32)
    vrs = A.alloc([NS_MAX], F32)
    dtraw = A.alloc([NS_MAX, NH], F32)
    dtt = A.alloc([NS_MAX, NH], F32)
    na = A.alloc([NS_MAX, NH], F32)
    na_hi = A.alloc([NS_MAX, NH], BF16)
    na_lo = A.alloc([NS_MAX, NH], BF16)
    nacum = A.alloc([NH], F32)
    dif = A.alloc([NH], F32)
    Ee = A.alloc([NH], F32)
    eend = A.alloc([NH], F32)
    dec = A.alloc([NH], F32)
    xdt = A.alloc([512], BF16)
    xD = A.alloc([512], F32)
    xw = A.alloc([512], BF16)
    btok = A.alloc([256], BF16)
    r1h = A.alloc([NS_MAX, NH, 128], BF16)
    r1l = A.alloc([NS_MAX, NH, 128], BF16)
    lexp = A.alloc([NH, 128], BF16)
    MT = A.alloc([NH, 128], BF16)
    cbm = A.alloc([2, 128], BF16)
    t1 = A.alloc([512], F32)
    yz = A.alloc([512], F32)
    junk, tz, vnf, vtmp = t1, t1, t1, yz
    ssq = A.alloc([2], F32)
    yn = A.alloc([512], BF16)
    stg = A.alloc([4, 128], F32, name="stg")
    bsb = A.alloc([2, 512], BF16, at=A.marks["stg"][0])
    NRING = (A.nb - A.off - 64) // 8192
    assert NRING >= 6, NRING
    NRM = 3
    NRF = min(NRING - NRM, 4)
    ringM = [A.alloc([4096], BF16) for _ in range(NRM)]
    ringF = [A.alloc([4096], BF16) for _ in range(NRF)]
    rm_i, rf_i, pm_i, pf_i = [0], [0], [0], [0]

    def wloadM(src_blk):
        buf = ringM[rm_i[0] % NRM]
        rm_i[0] += 1
        dma("sp", buf, src_blk)
        return buf

    def wloadF(src_blk):
        buf = ringF[rf_i[0] % NRF]
        rf_i[0] += 1
        dma("sp", buf, src_blk)
        return buf

    def psM(dt=F32):
        b = banks[pm_i[0] % 6]
        pm_i[0] += 1
        return b.bitcast(dt)

    def psF(dt=F32):
        b = banks[6 + pf_i[0] % 2]
        pf_i[0] += 1
        return b.bitcast(dt)

    ps = psM

    dma("sp", cf, consts)
    for i, v in enumerate((RMS_EPS, 4 * RMS_EPS, LN_EPS, 1.0)):
        mset("pool", epsb[:, i:i + 1], v)
    cp("dve", cb[:, 0:4, :], cf.rearrange("p (a b) -> p a b", a=4))
    ts("dve", ntri_b, tri_f, -1.0, None, ALU.mult)
    eps_rms, eps_rms4, eps_ln, one_c = epsb[:, 0:1], epsb[:, 1:2], epsb[:, 2:3], epsb[:, 3:4]

    def cast_gen(l):
        for b, c0 in enumerate((0, 512, 1024, 1544, 2056)):
            dma("pool", s_in[l][b].rearrange("p (k n) -> p k n", k=8),
                w_in[l][:, c0:c0 + 512].rearrange("(k p) n -> p k n", p=128))
            yield
        for b in range(2):
            dma("pool", s_out[l][b].rearrange("p (k n) -> p k n", k=8),
                w_out[l][:, b * 512:(b + 1) * 512].rearrange("(k p) n -> p k n", p=128))
            yield
        for b in range(8):
            dma("pool", s_f1[l][b].rearrange("p (k n) -> p k n", k=8),
                w_ff1[l][:, b * 512:(b + 1) * 512].rearrange("(k p) n -> p k n", p=128))
            yield
        for b in range(8):
            for k4 in range(4):
                dma("pool", s_f2[l][b].rearrange("p (k n) -> p k n", k=32)[:, k4 * 8:(k4 + 1) * 8, :],
                    w_ff2[l][k4 * 1024:(k4 + 1) * 1024, b * 128:(b + 1) * 128].rearrange("(k p) n -> p k n", p=128))
                yield

    def cast_weights(l):
        for _ in cast_gen(l):
            pass

    cast_weights(0)
    cast_q = []

    for l in range(L):
        dma("sp", dtb_bc[:, l, :], bc_row(dt_bias[l], NH))
        dma("sp", apos_bc[:, l, :], bc_row(a_log[l], NH))
        dma("sp", dsk_bc[:, l, :], bc_row(d_skip[l], NH))
        for k in range(4):
            dma("sp", convw[:, l, :, k], conv_w[l, k].rearrange("(m p) -> p m", p=128), slow=True)
        dma("sp", convb[:, l, :], conv_b[l].rearrange("(m p) -> p m", p=128), slow=True)
        for k in range(4):
            dma("sp", ng[:, l, k, :], norm_g[l, k].rearrange("(m p) -> p m", p=128), slow=True)
        dma("sp", bmod[:, l, :], b_mod[l].rearrange("(m p) -> p m", p=128), slow=True)
        dma("pool", wdt[:, l, :, :], w_in[l][:, 1536:1544].rearrange("(k p) n -> p k n", p=128))
    for q in range(NPS + 1):
        dma("sp", cT[:, :, q], cvec[q].rearrange("(m p) -> p m", p=128), slow=True)
    act(apos_bc, apos_bc, AF.Exp)
    ts("dve", convw, convw, 0.5, None, ALU.mult)
    ts("dve", convb, convb, 0.5, None, ALU.mult)
    act(cTs, cT, AF.Silu)
    for l in range(L):
        for g in range(4):
            dma("sp", stg[:, g, :], w_s[l, g])
        p = ps()
        for g in range(4):
            tr(p[:, g * 128:(g + 1) * 128], stg[:, g, :], ident_f)
        tt("dve", wsT[:, l, :, :], p[:, 0:512].rearrange("p (g i) -> p g i", g=4),
           tri_f.rearrange("p (o i) -> p o i", o=1).to_broadcast([128, 4, 128]), ALU.mult)
    for l in range(L):
        pm = ps()
        for jb in range(12):
            wb_ = wmb[jb % 2]
            dma("pool", wb_, w_mod[l][:, jb * 512:(jb + 1) * 512].rearrange("(k p) n -> p k n", p=128))
            for j in range(4):
                jj = jb * 4 + j
                for kc in range(KC):
                    mm(pm[:, jj * 3:(jj + 1) * 3], wb_[:, kc, j * 128:(j + 1) * 128], cTs[:, kc, :],
                       start=(kc == 0), stop=(kc == KC - 1))
        tt("dve", mod5[:, l].rearrange("p w k q -> p (w k) q"), pm[:, 0:144].rearrange("p (j q) -> p j q", q=3),
           bmod[:, l, :].rearrange("p (j o) -> p j o", o=1).to_broadcast([128, 48, 3]), ALU.add)
        for i, (w_sc, n_i) in enumerate(((1, 0), (4, 2))):
            stt("dve", gsc[:, l, i], mod5[:, l, w_sc], 1.0,
                ng[:, l, n_i, :].rearrange("p (k o) -> p k o", o=1).to_broadcast([128, 8, 3]), ALU.add, ALU.mult)
        for i, (w_g, n_i) in enumerate(((2, 1), (5, 3))):
            tt("dve", gn[:, l, i], mod5[:, l, w_g],
               ng[:, l, n_i, :].rearrange("p (k o) -> p k o", o=1).to_broadcast([128, 8, 3]), ALU.mult)

    def stats_rstd(ps, src_sq, rstd, TT, eps_ap, n):
        p = ps()
        for kc in range(KC):
            mm(p[:, :TT], ones_b, src_sq[:, kc, :TT], start=(kc == 0), stop=(kc == KC - 1))
        act(rstd[:, :TT], p[:, :TT], AF.Ln, bias=eps_ap, scale=1.0 / n)
        act(rstd[:, :TT], rstd[:, :TT], AF.Exp, scale=-0.5)

    def bcast_t(v, TT):
        return v[:, :TT].rearrange("p (o t) -> p o t", o=1).to_broadcast([128, 8, TT])

    def prenorm(ps, xT, hTt, sq, rstd, tmpA, l, q, which, TT):
        act(sq[:, :, :TT], xT[:, :, :TT], AF.Square)
        stats_rstd(ps, sq, rstd, TT, eps_rms, D)
        yield
        tt("dve", tmpA[:, :, :TT], xT[:, :, :TT], bcast_t(rstd, TT), ALU.mult)
        shw = 0 if which == 0 else 3
        for kc in range(KC):
            ts("pool", hTt[:, kc, :TT], tmpA[:, kc, :TT], gsc[:, l, which, kc, q:q + 1],
               mod5[:, l, shw, kc, q:q + 1], ALU.mult, ALU.add)
        yield

    def postnorm_residual(ps, xT, mT, sq, rstd, l, q, which, TT):
        stats_rstd(ps, sq, rstd, TT, eps_rms, D)
        yield
        tt("dve", mT[:, :, :TT], mT[:, :, :TT], bcast_t(rstd, TT), ALU.mult)
        for kc in range(KC):
            stt("dve", xT[:, kc, :TT], mT[:, kc, :TT], gn[:, l, which, kc, q:q + 1], xT[:, kc, :TT], ALU.mult, ALU.add)
        yield

    class Tile:
        pass

    def Mgen(tl, l):
        q, TT, Sb, NS, is_sample, xT = tl.q, tl.TT, tl.Sb, tl.NS, tl.is_sample, tl.xT
        ps, wload = psM, wloadM
        if l == 0:
            if tl.first:
                if is_sample:
                    for l2 in range(L):
                        for k in range(3):
                            dma("sp", ctail[:, l2, :, k], state_conv[l2, 0, k].rearrange("(m p) -> p m", p=128), slow=True)
                        dma("sp", stg[:, :, :], state_ssd[l2, 0].rearrange("(c p) n -> p c n", p=128))
                        p = ps()
                        for c in range(4):
                            tr(p[:, c * 128:(c + 1) * 128], stg[:, c, :], ident_f)
                        cp("act", hT[:, l2, :], p[:, 0:512])
                        cp("pool", hTb[:, l2, :], hT[:, l2, :])
                else:
                    mset("pool", hT[:, :, :], 0.0)
                    mset("pool", hTb[:, :, :], 0.0)
                    mset("pool", ctail[:, :, :, :], 0.0)
            dma("sp", xtok_in[:Sb, :NS, :], tl.xin[tl.t0:tl.t0 + TT, :].rearrange("(s p) d -> p s d", p=Sb))
            yield
            for kc in range(KC):
                p = ps()
                for s in range(NS):
                    tr(p[:, s * Sb:(s + 1) * Sb], xtok_in[:Sb, s, kc * 128:(kc + 1) * 128], ident_f[:Sb, :Sb])
                cp("act" if kc % 2 == 0 else "dve", xT[:, kc, :TT], p[:, :TT])
                if kc % 2 == 1:
                    yield
        dma("sp", lpb[:, 0, :], bc_row(ssd_norm_g[l], 512))
        dma("sp", lpb[:, 1, :], bc_row(v_ln_g[l], 512))
        dma("sp", lpb[:, 2, :], bc_row(v_ln_b[l], 512))
        dma("sp", lpb[0:1, 3, :], b_s[l].rearrange("(o n) -> o n", o=1))
        ssdg_bc, vg_bc, vb_bc, bsrow = lpb[:, 0, :], lpb[:, 1, :], lpb[:, 2, :], lpb[0:1, 3, :]
        cp("pool", bsb[0:1, 0, :], bsrow)
        tt("pool", bsb[0:1, 1, :], bsrow, bsb[0:1, 0, :], ALU.subtract)
        for _ in prenorm(ps, xT, hTt_m, sq_m, rstd_m, xbc_pre[:, :, 0:T], l, q, 0, TT):
            yield 2
        hTt = hTt_m
        W = wload(s_in[l][0]).rearrange("p (k n) -> p k n", k=8)
        for s in range(NS):
            p = ps()
            for kc in range(KC):
                mm(p[:Sb, :512], hTt[:, kc, s * Sb:(s + 1) * Sb], W[:, kc, :], start=(kc == 0), stop=(kc == KC - 1))
            act(tz[:Sb, :], p[:Sb, :512], AF.Tanh, scale=0.5)
            stt("dve", sz[:Sb, s, :], tz[:Sb, :], 1.0, p[:Sb, :512], ALU.add, ALU.mult)
            yield
        cp("pool", xbc_pre[:, :, 0:3], ctail[:, l, :, :])

        def conv_fin(m_):
            act(th[m_ % 3][:, :TT], acc[m_ % 3][:, :TT], AF.Tanh)
            tt("pool", th[m_ % 3][:, :TT], th[m_ % 3][:, :TT], acc[m_ % 3][:, :TT], ALU.mult)
            tt("pool", xbcT[:, m_, :TT], th[m_ % 3][:, :TT], acc[m_ % 3][:, :TT], ALU.add)

        Wx = wload(s_in[l][1]).rearrange("p (k n) -> p k n", k=8)
        Wb = wload(s_in[l][2]).rearrange("p (k n) -> p k n", k=8)
        for m in range(8):
            Wm = Wx if m < 4 else Wb
            p = ps()
            for kc in range(KC):
                mm(p[:, :TT], Wm[:, kc, (m % 4) * 128:(m % 4 + 1) * 128], hTt[:, kc, :TT],
                   start=(kc == 0), stop=(kc == KC - 1))
            cp("act", xbc_pre[:, m, 3:3 + TT], p[:, :TT])
            a_ = acc[m % 3]
            act(a_[:, :TT], p[:, :TT], AF.Identity, bias=convb[:, l, m:m + 1], scale=convw[:, l, m, 3:4])
            if m >= 2:
                conv_fin(m - 2)
            for k in range(0, 3):
                stt("dve", a_[:, :TT], xbc_pre[:, m, k:k + TT], convw[:, l, m, k:k + 1], a_[:, :TT], ALU.mult, ALU.add)
            yield m % 2
        conv_fin(6)
        conv_fin(7)
        cp("pool", ctail[:, l, :, :], xbc_pre[:, :, TT:TT + 3])
        if tl.last:
            for k in range(3):
                dma("sp", tl.conv_out[l, k].rearrange("(m p) -> p m", p=128), ctail[:, l, :, k], slow=True, is_out=True)
        Wu = wload(s_in[l][3]).rearrange("p (k n) -> p k n", k=8)
        for g in range(4):
            p = ps()
            for kc in range(KC):
                mm(p[:, :TT], Wu[:, kc, g * 128:(g + 1) * 128], hTt[:, kc, :TT], start=(kc == 0), stop=(kc == KC - 1))
            act(uT[:, g, :TT], p[:, :TT], AF.Gelu_apprx_tanh)
            if g % 2 == 1:
                yield
        Wv = wload(s_in[l][4]).rearrange("p (k n) -> p k n", k=8)
        for s in range(NS):
            p = ps()
            for kc in range(KC):
                mm(p[:Sb, :512], hTt[:, kc, s * Sb:(s + 1) * Sb], Wv[:, kc, :], start=(kc == 0), stop=(kc == KC - 1))
            act(vg[:Sb, s, :], p[:Sb, :512], AF.Gelu_apprx_tanh)
            S.add("dve", (lambda s=s: nc.vector.bn_stats(out=vst[:Sb, s, :], in_=vg[:Sb, s, :])),
                  reads=[vg[:Sb, s, :]], writes=[vst[:Sb, s, :]])
            S.add("dve", (lambda s=s: nc.vector.bn_aggr(out=vmv[:Sb, s, :], in_=vst[:Sb, s, :])),
                  reads=[vst[:Sb, s, :]], writes=[vmv[:Sb, s, :]])
            yield
        pd = ps()
        for s in range(NS):
            for kc in range(KC):
                mm(pd[:Sb, s * NH:(s + 1) * NH], hTt[:, kc, s * Sb:(s + 1) * Sb], wdt[:, l, kc, :],
                   start=(kc == 0), stop=(kc == KC - 1))
        tt("dve", dtraw[:Sb, :NS, :], pd[:Sb, :NS * NH].rearrange("p (s h) -> p s h", h=NH),
           dtb_bc[:Sb, l, :].rearrange("p (o h) -> p o h", o=1).to_broadcast([Sb, NS, NH]), ALU.add)
        yield
        act(dtt[:Sb, :NS, :], dtraw[:Sb, :NS, :], AF.Exp)
        act(dtt[:Sb, :NS, :], dtt[:Sb, :NS, :], AF.Ln, bias=one_c[:Sb, :])
        tt("dve", na[:Sb, :NS, :], dtt[:Sb, :NS, :],
           apos_bc[:Sb, l, :].rearrange("p (o h) -> p o h", o=1).to_broadcast([Sb, NS, NH]), ALU.mult)
        cp("pool", na_hi[:Sb, :NS, :], na[:Sb, :NS, :])
        tt("dve", na_lo[:Sb, :NS, :], na[:Sb, :NS, :], na_hi[:Sb, :NS, :], ALU.subtract)
        for s in range(NS):
            for (r1, src) in ((r1h, na_hi), (r1l, na_lo)):
                tt("dve", r1[:Sb, s, :, :Sb], src[:Sb, s, :].rearrange("p (h o) -> p h o", o=1).to_broadcast([Sb, NH, Sb]),
                   ntri_b[:Sb, :Sb].rearrange("p (o i) -> p o i", o=1).to_broadcast([Sb, NH, Sb]), ALU.mult)
        act(vrs[:Sb, :NS], vmv[:Sb, :NS, 1], AF.Ln, bias=eps_ln[:Sb, :])
        act(vrs[:Sb, :NS], vrs[:Sb, :NS], AF.Exp, scale=-0.5)
        yield
        for s in range(NS):
            ts("dve", vtmp[:Sb, :], vg[:Sb, s, :], vmv[:Sb, s, 0:1], vrs[:Sb, s:s + 1], ALU.subtract, ALU.mult)
            tt("pool", vtmp[:Sb, :], vtmp[:Sb, :], vg_bc[:Sb, :], ALU.mult)
            if is_sample:
                tt("pool", vnf[:Sb, :], vtmp[:Sb, :], vb_bc[:Sb, :], ALU.add)
                cp("pool", vn[:Sb, s, :], vnf[:Sb, :])
                dma("sp", v_sample[l, 0, :, :], vnf[:Sb, :], is_out=True)
            else:
                tt("pool", vn[:Sb, s, :], vtmp[:Sb, :], vb_bc[:Sb, :], ALU.add)
        yield
        for s in range(NS):
            tok = slice(s * Sb, (s + 1) * Sb)
            pT = ps(BF16)
            for c in range(6):
                tr(pT[:Sb, c * 128:(c + 1) * 128], xbcT[:, c, tok], ident_b)
            pc = ps()
            mm(pc[:Sb, 0:NH], tri_f[:Sb, :Sb], na[:Sb, s, :])
            mm(pc[:, NH:2 * NH], ones_f[:Sb, :], na[:Sb, s, :])
            segs = []
            for hf in range(2):
                pg = ps()
                o = pg[:Sb, :4 * Sb]
                hs = slice(4 * hf, 4 * hf + 4)
                mm(o, ones_b[:Sb, :Sb], r1h[:Sb, s, hs, :Sb], start=True, stop=False)
                mm(o, ones_b[:Sb, :Sb], r1l[:Sb, s, hs, :Sb], start=False, stop=False)
                mm(o, tri_b[:Sb, :Sb], na_hi[:Sb, s, hs].rearrange("p (h o) -> p h o", o=1).to_broadcast([Sb, 4, Sb]),
                   start=False, stop=False)
                mm(o, tri_b[:Sb, :Sb], na_lo[:Sb, s, hs].rearrange("p (h o) -> p h o", o=1).to_broadcast([Sb, 4, Sb]),
                   start=False, stop=False)
                mm(o, ident_b[:Sb, :Sb], negm_b[:Sb, :Sb].rearrange("p (o i) -> p o i", o=1).to_broadcast([Sb, 4, Sb]),
                   start=False, stop=True)
                segs.append(pg)
            pcb = ps()
            for g in range(2):
                mm(pcb[:Sb, g * Sb:(g + 1) * Sb], xbcT[:, 4 + g, tok], xbcT[:, 6 + g, tok])
            yield 2
            tt("dve", xdt[:Sb, :].rearrange("p (h d) -> p h d", h=NH), pT[:Sb, 0:512].rearrange("p (h d) -> p h d", h=NH),
               dtt[:Sb, s, :].rearrange("p (h o) -> p h o", o=1).to_broadcast([Sb, NH, HD]), ALU.mult)
            tt("dve", cbm[:Sb, :, :Sb], pcb[:Sb, :2 * Sb].rearrange("p (g i) -> p g i", g=2),
               tri_f[:Sb, :Sb].rearrange("p (o i) -> p o i", o=1).to_broadcast([Sb, 2, Sb]), ALU.mult)
            for hf in range(2):
                act(lexp[:Sb, 4 * hf:4 * hf + 4, :Sb], segs[hf][:Sb, :4 * Sb].rearrange("p (h i) -> p h i", h=4), AF.Exp)
                tt("dve", MT[:Sb, 4 * hf:4 * hf + 4, :Sb], lexp[:Sb, 4 * hf:4 * hf + 4, :Sb],
                   cbm[:Sb, hf, :Sb].rearrange("p (o i) -> p o i", o=1).to_broadcast([Sb, 4, Sb]), ALU.mult)
            cp("act", nacum[:Sb, :], pc[:Sb, 0:NH])
            act(dec[:, :], pc[:, NH:2 * NH], AF.Exp, scale=-1.0)
            tt("dve", dif[:Sb, :], pc[:Sb, NH:2 * NH], nacum[:Sb, :], ALU.subtract)
            act(Ee[:Sb, :], nacum[:Sb, :], AF.Exp, scale=-1.0)
            act(eend[:Sb, :], dif[:Sb, :], AF.Exp, scale=-1.0)
            cp("act", btok[:Sb, :], pT[:Sb, 512:768])
            tt("dve", xD[:Sb, :].rearrange("p (h d) -> p h d", h=NH), pT[:Sb, 0:512].rearrange("p (h d) -> p h d", h=NH),
               dsk_bc[:Sb, l, :].rearrange("p (h o) -> p h o", o=1).to_broadcast([Sb, NH, HD]), ALU.mult)
            yield 1
            py = ps()
            for h in range(NH):
                mm(py[:Sb, h * HD:(h + 1) * HD], MT[:Sb, h, :Sb], xdt[:Sb, h * HD:(h + 1) * HD])
            pyo = ps()
            for g in range(2):
                mm(pyo[:Sb, g * 256:(g + 1) * 256], xbcT[:, 6 + g, tok], hTb[:, l, g * 256:(g + 1) * 256])
            tt("pool", xw[:Sb, :].rearrange("p (h d) -> p h d", h=NH), xdt[:Sb, :].rearrange("p (h d) -> p h d", h=NH),
               eend[:Sb, :].rearrange("p (h o) -> p h o", o=1).to_broadcast([Sb, NH, HD]), ALU.mult)
            pst = ps()
            for g in range(2):
                mm(pst[:, g * 256:(g + 1) * 256], btok[:Sb, g * 128:(g + 1) * 128], xw[:Sb, g * 256:(g + 1) * 256])
            pmx = ps()
            for g in range(4):
                mm(pmx[:, g * Sb:(g + 1) * Sb], vn[:Sb, s, g * 128:(g + 1) * 128], wsT[:Sb, l, g, :Sb], start=True, stop=False)
                mm(pmx[:, g * Sb:(g + 1) * Sb], ones_b[0:1, :], bsb[0:1, 0, g * 128:g * 128 + Sb], start=False, stop=False)
                mm(pmx[:, g * Sb:(g + 1) * Sb], ones_b[0:1, :], bsb[0:1, 1, g * 128:g * 128 + Sb], start=False, stop=True)
            yield 2
            tt("dve", t1[:Sb, :].rearrange("p (h d) -> p h d", h=NH), pyo[:Sb, 0:512].rearrange("p (h d) -> p h d", h=NH),
               Ee[:Sb, :].rearrange("p (h o) -> p h o", o=1).to_broadcast([Sb, NH, HD]), ALU.mult)
            tt("dve", t1[:Sb, :], py[:Sb, 0:512], t1[:Sb, :], ALU.add)
            tt("pool", t1[:Sb, :], t1[:Sb, :], xD[:Sb, :], ALU.add)
            yield 1
            tt("pool", yz[:Sb, :], t1[:Sb, :], sz[:Sb, s, :], ALU.mult)
            yield 1
            act(junk[:Sb, :], yz[:Sb, :], AF.Square, accum=ssq[:Sb, 0:1])
            act(ssq[:Sb, 1:2], ssq[:Sb, 0:1], AF.Ln, bias=eps_rms4[:Sb, :], scale=1.0 / SSDW)
            act(ssq[:Sb, 1:2], ssq[:Sb, 1:2], AF.Exp, scale=-0.5)
            yield 1
            stt("dve", yn[:Sb, :], yz[:Sb, :], ssq[:Sb, 1:2], ssdg_bc[:Sb, :], ALU.mult, ALU.mult)
            tt("dve", catT[:, 4:8, tok], pmx[:, :4 * Sb].rearrange("p (g i) -> p g i", g=4), uT[:, :, tok], ALU.mult)
            tt("dve", hT[:, l, :].rearrange("p (h d) -> p h d", h=NH), hT[:, l, :].rearrange("p (h d) -> p h d", h=NH),
               dec[:, :].rearrange("p (h o) -> p h o", o=1).to_broadcast([128, NH, HD]), ALU.mult)
            tt("dve", hT[:, l, :], pst[:, 0:512], hT[:, l, :], ALU.add)
            cp("pool", hTb[:, l, :], hT[:, l, :])
            yield 2
            pT2 = ps(BF16)
            for c in range(4):
                tr(pT2[:, c * Sb:(c + 1) * Sb], yn[:Sb, c * 128:(c + 1) * 128], ident_b[:Sb, :Sb])
            cp("act", catT[:, 0:4, tok], pT2[:, :4 * Sb].rearrange("p (c i) -> p c i", c=4))
            yield 1
        if tl.last:
            p = ps()
            for c in range(4):
                tr(p[:, c * 128:(c + 1) * 128], hT[:, l, c * 128:(c + 1) * 128], ident_f)
            cp("act", stg[:, :, :], p[:, 0:512].rearrange("p (c n) -> p c n", c=4))
            dma("sp", tl.ssd_out[l].rearrange("(c p) n -> p c n", p=128), stg[:, :, :], is_out=True)
        for b in range(2):
            Wo = wload(s_out[l][b]).rearrange("p (k n) -> p k n", k=8)
            for mi in range(4):
                m = b * 4 + mi
                p = ps()
                for kc in range(KC):
                    mm(p[:, :TT], Wo[:, kc, mi * 128:(mi + 1) * 128], catT[:, kc, :TT], start=(kc == 0), stop=(kc == KC - 1))
                cp("act", mT_m[:, m, :TT], p[:, :TT])
                act(sq_m[:, m, :TT], p[:, :TT], AF.Square)
                if mi % 2 == 1:
                    yield
        for _ in postnorm_residual(ps, xT, mT_m, sq_m, rstd_m, l, q, 0, TT):
            yield 3
        for _ in prenorm(ps, xT, hTt_f, sq_m, rstd_m, mT_m, l, q, 1, TT):
            yield 3

    def Fgen(tl, l):
        q, TT, Sb, NS, xT = tl.q, tl.TT, tl.Sb, tl.NS, tl.xT
        ps, wload = psF, wloadF
        for b in range(8):
            W1 = wload(s_f1[l][b]).rearrange("p (k n) -> p k n", k=8)
            for hi in range(4):
                hc = b * 4 + hi
                p = ps()
                for kc in range(KC):
                    mm(p[:, :TT], W1[:, kc, hi * 128:(hi + 1) * 128], hTt_f[:, kc, :TT], start=(kc == 0), stop=(kc == KC - 1))
                r_ = rl[hc % 3]
                act(r_[:, :TT], p[:, :TT], AF.Relu)
                tt("pool", hid[:, hc, :TT], r_[:, :TT], r_[:, :TT], ALU.mult)
                yield
        for m in range(8):
            W2 = wload(s_f2[l][m]).rearrange("p (k n) -> p k n", k=32)
            p = ps()
            for kc in range(32):
                mm(p[:, :TT], W2[:, kc, :], hid[:, kc, :TT], start=(kc == 0), stop=(kc == 31))
                if kc % 8 == 7 and kc != 31:
                    yield
            cp("act", mT_f[:, m, :TT], p[:, :TT])
            act(sq_f[:, m, :TT], p[:, :TT], AF.Square)
            yield
        yield from postnorm_residual(ps, xT, mT_f, sq_f, rstd_f, l, q, 1, TT)
        if l == L - 1:
            for s in range(NS):
                for half in range(2):
                    p = ps()
                    for k4 in range(4):
                        kc = half * 4 + k4
                        tr(p[:Sb, k4 * 128:(k4 + 1) * 128], xT[:, kc, s * Sb:(s + 1) * Sb], ident_f)
                    cp("act" if half == 0 else "dve", xtok_out[:Sb, s, half * 512:(half + 1) * 512], p[:Sb, 0:512])
                yield
            dma("sp", tl.yout[tl.t0:tl.t0 + TT, :].rearrange("(s p) d -> p s d", p=Sb), xtok_out[:Sb, :NS, :], is_out=True)
            yield

    tiles = []
    streams = [("p", i) for i in range(NPS)] + [("s", 0)]
    for q, (kind, bi) in enumerate(streams):
        if kind == "s":
            Ltok, TT, Sb, NS = DSQ, DSQ, DSQ, 1
            xin, yout, conv_out, ssd_out = x_sample[0], y_sample[0], conv_sample[:, 0], ssd_sample[:, 0]
        else:
            Ltok, TT, Sb, NS = SEQ, T, 128, T // 128
            xin, yout, conv_out, ssd_out = x_prompt[bi], y_prompt[bi], conv_prompt[:, bi], ssd_prompt[:, bi]
        nt = Ltok // TT
        for ti in range(nt):
            tl = Tile()
            tl.q, tl.TT, tl.Sb, tl.NS, tl.is_sample = q, TT, Sb, NS, kind == "s"
            tl.xin, tl.yout, tl.conv_out, tl.ssd_out = xin, yout, conv_out, ssd_out
            tl.t0, tl.first, tl.last = ti * TT, ti == 0, ti == nt - 1
            tl.xT = xT_smp if kind == "s" else xTs[len(tiles) % 2]
            tiles.append(tl)

    def drive(gens):
        gf, gm = gens if len(gens) == 2 else (gens[0], None)
        f_done = gf is None

        def fstep():
            nonlocal f_done
            if not f_done:
                try:
                    next(gf)
                except StopIteration:
                    f_done = True

        if gm is not None:
            for _ in range(4):
                fstep()
            for k in gm:
                if cast_q:
                    try:
                        next(cast_q[0])
                    except StopIteration:
                        cast_q.pop(0)
                for _ in range(1 if k is None else k):
                    fstep()
        while not f_done:
            fstep()

    order = []
    groups = [tiles[g0:g0 + 2] for g0 in range(0, len(tiles), 2)]
    for grp in groups:
        for l in range(L):
            for tl in grp:
                order.append((tl, l))
    pending = None
    cast_done = {0}
    for (tl, l) in order:
        if tl is tiles[0] and l + 1 < L and (l + 1) not in cast_done:
            for g_ in cast_q:
                for _ in g_:
                    pass
            del cast_q[:]
            cast_q.append(cast_gen(l + 1))
            cast_done.add(l + 1)
        if pending is not None and pending[0] is tl:
            drive([Fgen(*pending)])
            pending = None
        drive([Fgen(*pending) if pending is not None else None, Mgen(tl, l)])
        pending = (tl, l)
    drive([Fgen(*pending)])

    sem_names = [("c", e) for e in ("pe", "act", "dve", "pool")]
    for qn, k in S.dma_slots.items():
        sem_names += [(qn, i) for i in range(k)]
    import contextlib
    with contextlib.ExitStack() as st:
        sems = {}
        for i, sn in enumerate(sem_names):
            sems[sn] = st.enter_context(nc.semaphore("sem%d" % i))
        block = st.enter_context(nc.Block())

        @block.sync
        def _(sync):
            S.emit(sems)
    return nc, S


_OUT_NAMES = ["y_prompt", "y_sample", "conv_prompt", "ssd_prompt", "conv_sample", "ssd_sample", "v_sample"]


def run(cfg, inputs, trace=False):
    L, NPS, NCO = cfg.depth, cfg.nps, cfg.ncores
    nc, S = build(cfg)
    f = lambda a: np.ascontiguousarray(np.asarray(a, dtype=np.float32))
    shared = {k: f(inputs[k])[:L] for k in ("w_mod", "b_mod", "norm_g", "w_in", "conv_w", "conv_b", "dt_bias", "a_log",
                                             "d_skip", "ssd_norm_g", "v_ln_g", "v_ln_b", "w_s", "w_out", "w_ff1", "w_ff2")}
    shared["b_s"] = f(inputs["b_s"])[:L].reshape(L, 512)
    shared["consts"] = consts_host()
    xp, xs = f(inputs["x_prompt"]), f(inputs["x_sample"])
    sc, ss = f(inputs["state_conv"]), f(inputs["state_ssd"])
    cpv, csv = f(inputs["c_prompt"]), f(inputs["c_sample"])
    in_maps = []
    for c in range(NCO):
        m = dict(shared)
        m["x_prompt"] = xp[c * NPS:(c + 1) * NPS, :cfg.seq]
        m["x_sample"] = xs[c:c + 1, :cfg.dseq]
        m["state_conv"] = np.ascontiguousarray(sc[:L, c:c + 1])
        m["state_ssd"] = np.ascontiguousarray(ss[:L, c:c + 1]).reshape(L, 1, NH * HD, NST)
        m["cvec"] = np.ascontiguousarray(np.concatenate([cpv[c * NPS:(c + 1) * NPS], csv[c:c + 1]], axis=0))
        in_maps.append(m)
    res = run_bass_kernel_spmd(nc, in_maps, core_ids=list(range(NCO)), trace=trace)
    R = res.results
    outs = (
        np.concatenate([r["y_prompt"] for r in R], axis=0),
        np.concatenate([r["y_sample"] for r in R], axis=0),
        np.concatenate([r["conv_prompt"] for r in R], axis=1),
        np.concatenate([r["ssd_prompt"] for r in R], axis=1).reshape(L, NCO * NPS, NH, HD, NST),
        np.concatenate([r["conv_sample"] for r in R], axis=1),
        np.concatenate([r["ssd_sample"] for r in R], axis=1).reshape(L, NCO, NH, HD, NST),
        np.concatenate([r["v_sample"] for r in R], axis=1),
    )
    return tuple(np.ascontiguousarray(o, dtype=np.float32) for o in outs), res


def kernel(**inputs):
    cfg = Cfg()
    outs, _ = run(cfg, inputs)
    return outs
```
